# Optimizing a Trainium2 kernel written in Bass

```python
import math
import jax, jax.numpy as jnp
from jax import lax
import numpy as np

D_MODEL = 2048
BATCH = 4
SEQ = 4096
DEPTH = 2

MIX_W = D_MODEL // 2
M_HEADS = 4
M_DH = MIX_W // M_HEADS
M_CONV = 4
M_CHUNK = 128
DA_HEADS = 8
DA_DH = MIX_W // (2 * DA_HEADS)
SB_HEADS = 8
SB_DH = MIX_W // SB_HEADS
Q_BLOCK = 128
N_BRANCH = 3
D_FF = ((8 * D_MODEL // 3 + 255) // 256) * 256
FF_CONV = 3
N_MOD = 6
EPS = 1e-6
IN_SIZES = (MIX_W, MIX_W, MIX_W, MIX_W, M_HEADS, M_HEADS,
            MIX_W, MIX_W, MIX_W,
            MIX_W, MIX_W, MIX_W,
            D_MODEL, D_MODEL, D_MODEL)
N_IN = sum(IN_SIZES)

kernel_name = "hybrid_mlstm_diffattn_stickbreak_convffn"


def rmsnorm(x, g):
    xf = x.astype(jnp.float32)
    y = xf * lax.rsqrt(jnp.mean(xf * xf, axis=-1, keepdims=True) + EPS)
    return (y * g.astype(jnp.float32)).astype(x.dtype)


def causal_dwconv(x, w):
    k = w.shape[0]
    return lax.conv_general_dilated(
        x, w[:, None, :].astype(x.dtype), window_strides=(1,), padding=[(k - 1, 0)],
        dimension_numbers=('NWC', 'WIO', 'NWC'), feature_group_count=x.shape[-1])


def split_cols(z, sizes):
    outs, off = [], 0
    for s in sizes:
        outs.append(z[..., off:off + s])
        off += s
    return outs


def to_blocks(t, axis, blk):
    shp = t.shape
    t = t.reshape(shp[:axis] + (shp[axis] // blk, blk) + shp[axis + 1:])
    return jnp.moveaxis(t, axis, 0)


def mlstm_chunkwise(q, k, v, i_pre, f_pre):
    b_, h_, s_, d = q.shape
    q = q * (d ** -0.5)
    logf = jax.nn.log_sigmoid(f_pre)
    xs = (to_blocks(q, 2, M_CHUNK), to_blocks(k, 2, M_CHUNK), to_blocks(v, 2, M_CHUNK),
          to_blocks(i_pre, 2, M_CHUNK), to_blocks(logf, 2, M_CHUNK))
    causal = jnp.tril(jnp.ones((M_CHUNK, M_CHUNK), dtype=bool))

    def step(carry, inp):
        C, n, m = carry
        qc, kc, vc, ic, fc = inp
        b = jnp.cumsum(fc, axis=-1)
        dmat = jnp.where(causal, b[..., :, None] - b[..., None, :] + ic[..., None, :], -jnp.inf)
        m_inter = b + m[..., None]
        m_t = jnp.maximum(jnp.max(dmat, axis=-1), m_inter)
        w = jnp.einsum('bhtd,bhsd->bhts', qc, kc) * jnp.exp(dmat - m_t[..., None])
        decay = jnp.exp(m_inter - m_t)
        num = jnp.einsum('bhts,bhsd->bhtd', w, vc) + decay[..., None] * jnp.einsum('bhtd,bhde->bhte', qc, C)
        den = jnp.sum(w, axis=-1) + decay * jnp.einsum('bhtd,bhd->bht', qc, n)
        h = num / jnp.maximum(jnp.abs(den), jnp.exp(-m_t))[..., None]
        b_last = b[..., -1]
        g = b_last[..., None] - b + ic
        m_new = jnp.maximum(b_last + m, jnp.max(g, axis=-1))
        wk = jnp.exp(g - m_new[..., None])
        carry_decay = jnp.exp(b_last + m - m_new)
        C = carry_decay[..., None, None] * C + jnp.einsum('bhs,bhsd,bhse->bhde', wk, kc, vc)
        n = carry_decay[..., None] * n + jnp.einsum('bhs,bhsd->bhd', wk, kc)
        return (C, n, m_new), h

    init = (jnp.zeros((b_, h_, d, d), jnp.float32), jnp.zeros((b_, h_, d), jnp.float32),
            jnp.zeros((b_, h_), jnp.float32))
    _, hs = lax.scan(step, init, xs)
    return jnp.moveaxis(hs, 0, 2).reshape(b_, h_, s_, d)


def differential_attention(q, k, v, lam):
    b_, h_, _, s_, dh = q.shape
    qb = to_blocks(q, 3, Q_BLOCK)
    key_pos = jnp.arange(s_)

    def block(args):
        qi, idx = args
        q_pos = idx * Q_BLOCK + jnp.arange(Q_BLOCK)
        s = jnp.einsum('bhcqd,bhckd->bhcqk', qi, k).astype(jnp.float32)
        s = jnp.where(key_pos[None, :] <= q_pos[:, None], s, -jnp.inf)
        p = jax.nn.softmax(s, axis=-1)
        a = p[:, :, 0] - lam * p[:, :, 1]
        return jnp.einsum('bhqk,bhkd->bhqd', a.astype(v.dtype), v)

    out = lax.map(block, (qb, jnp.arange(s_ // Q_BLOCK)))
    return jnp.moveaxis(out, 0, 2).reshape(b_, h_, s_, v.shape[-1])


def stick_breaking_attention(q, k, v):
    b_, h_, s_, d = q.shape
    scale = d ** -0.5
    qb = to_blocks(q, 2, Q_BLOCK)
    key_pos = jnp.arange(s_)

    def block(args):
        qi, idx = args
        q_pos = idx * Q_BLOCK + jnp.arange(Q_BLOCK)
        z = jnp.einsum('bhqd,bhkd->bhqk', qi, k).astype(jnp.float32) * scale
        mask = key_pos[None, :] < q_pos[:, None]
        log_1m = jnp.where(mask, jax.nn.log_sigmoid(-z), 0.0)
        suffix = lax.cumsum(log_1m, axis=3, reverse=True) - log_1m
        a = jnp.where(mask, jnp.exp(jax.nn.log_sigmoid(z) + suffix), 0.0)
        return jnp.einsum('bhqk,bhkd->bhqd', a.astype(v.dtype), v)

    out = lax.map(block, (qb, jnp.arange(s_ // Q_BLOCK)))
    return jnp.moveaxis(out, 0, 2).reshape(b_, h_, s_, d)


def setup_inputs(seed: int = 0) -> dict:
    key = jax.random.key(seed)
    ks = jax.random.split(key, 24)
    nrm = jax.random.normal
    L, D = DEPTH, D_MODEL
    return {
        "x": nrm(ks[0], (BATCH, SEQ, D), jnp.float32),
        "c": nrm(ks[1], (BATCH, D), jnp.float32),
        "w_ada": nrm(ks[2], (L, D, N_MOD * D), jnp.float32) * (0.5 * D ** -0.5),
        "b_ada": nrm(ks[3], (L, N_MOD * D), jnp.float32) * 0.02,
        "g_mix": 1.0 + 0.02 * nrm(ks[4], (L, D), jnp.float32),
        "g_ffn": 1.0 + 0.02 * nrm(ks[5], (L, D), jnp.float32),
        "w_in": nrm(ks[6], (L, D, N_IN), jnp.float32) * D ** -0.5,
        "b_gate_if": jnp.concatenate([0.1 * nrm(ks[7], (L, M_HEADS), jnp.float32),
                                      3.0 + 0.5 * nrm(ks[8], (L, M_HEADS), jnp.float32)], axis=-1),
        "w_mconv": nrm(ks[9], (L, M_CONV, 2 * MIX_W), jnp.float32) * M_CONV ** -0.5,
        "g_mout": 1.0 + 0.02 * nrm(ks[10], (L, M_HEADS, M_DH), jnp.float32),
        "g_dq": 1.0 + 0.02 * nrm(ks[11], (L, DA_DH), jnp.float32),
        "g_dk": 1.0 + 0.02 * nrm(ks[12], (L, DA_DH), jnp.float32),
        "lam_q1": 0.1 * nrm(ks[13], (L, DA_DH), jnp.float32),
        "lam_k1": 0.1 * nrm(ks[14], (L, DA_DH), jnp.float32),
        "lam_q2": 0.1 * nrm(ks[15], (L, DA_DH), jnp.float32),
        "lam_k2": 0.1 * nrm(ks[16], (L, DA_DH), jnp.float32),
        "g_dsub": 1.0 + 0.02 * nrm(ks[17], (L, 2 * DA_DH), jnp.float32),
        "w_branch": nrm(ks[18], (L, N_BRANCH, MIX_W, D), jnp.float32) * MIX_W ** -0.5,
        "w_out": nrm(ks[19], (L, D, D), jnp.float32) * D ** -0.5,
        "w_up": nrm(ks[20], (L, D, 2 * D_FF), jnp.float32) * D ** -0.5,
        "w_ffconv": nrm(ks[21], (L, FF_CONV, D_FF), jnp.float32) * FF_CONV ** -0.5,
        "w_down": nrm(ks[22], (L, D_FF, D), jnp.float32) * D_FF ** -0.5,
    }


def reference(x, c, w_ada, b_ada, g_mix, g_ffn, w_in, b_gate_if, w_mconv, g_mout, g_dq, g_dk,
              lam_q1, lam_k1, lam_q2, lam_k2, g_dsub, w_branch, w_out, w_up, w_ffconv, w_down):
    B, S, D = x.shape
    for l in range(DEPTH):
        mod = (jax.nn.silu(c) @ w_ada[l] + b_ada[l]).reshape(B, N_MOD, 1, D)
        sh1, sc1, gt1, sh2, sc2, gt2 = (mod[:, j] for j in range(N_MOD))

        h = rmsnorm(x, g_mix[l]) * (1.0 + sc1) + sh1
        z = h @ w_in[l]
        (mq, mk, mv, mo, mi, mf, dq, dk, dv, sq, sk, sv,
         gm, gd, gs) = split_cols(z, IN_SIZES)

        mqk = jax.nn.silu(causal_dwconv(jnp.concatenate([mq, mk], axis=-1), w_mconv[l]))
        mq, mk = mqk[..., :MIX_W], mqk[..., MIX_W:]
        heads_m = lambda t: t.reshape(B, S, M_HEADS, M_DH).transpose(0, 2, 1, 3).astype(jnp.float32)
        i_pre = (mi + b_gate_if[l, :M_HEADS]).transpose(0, 2, 1).astype(jnp.float32)
        f_pre = (mf + b_gate_if[l, M_HEADS:]).transpose(0, 2, 1).astype(jnp.float32)
        hm = mlstm_chunkwise(heads_m(mq), heads_m(mk), heads_m(mv), i_pre, f_pre)
        hm = rmsnorm(hm.transpose(0, 2, 1, 3), g_mout[l]).astype(x.dtype)
        out_m = hm.reshape(B, S, MIX_W) * jax.nn.sigmoid(mo)

        lam_init = 0.8 - 0.6 * math.exp(-0.3 * l)
        lam = (jnp.exp(jnp.sum(lam_q1[l] * lam_k1[l])) - jnp.exp(jnp.sum(lam_q2[l] * lam_k2[l]))
               + lam_init).astype(jnp.float32)
        qd = rmsnorm(dq.reshape(B, S, DA_HEADS, 2, DA_DH), g_dq[l]) * (DA_DH ** -0.5)
        kd = rmsnorm(dk.reshape(B, S, DA_HEADS, 2, DA_DH), g_dk[l])
        vd = dv.reshape(B, S, DA_HEADS, 2 * DA_DH).transpose(0, 2, 1, 3)
        od = differential_attention(qd.transpose(0, 2, 3, 1, 4), kd.transpose(0, 2, 3, 1, 4), vd, lam)
        od = rmsnorm(od.transpose(0, 2, 1, 3), g_dsub[l]) * (1.0 - lam_init)
        out_d = od.reshape(B, S, MIX_W)

        heads_s = lambda t: t.reshape(B, S, SB_HEADS, SB_DH).transpose(0, 2, 1, 3)
        osb = stick_breaking_attention(heads_s(sq), heads_s(sk), heads_s(sv))
        out_s = osb.transpose(0, 2, 1, 3).reshape(B, S, MIX_W)

        merged = (jax.nn.sigmoid(gm) * (out_m @ w_branch[l, 0])
                  + jax.nn.sigmoid(gd) * (out_d @ w_branch[l, 1])
                  + jax.nn.sigmoid(gs) * (out_s @ w_branch[l, 2]))
        x = x + gt1 * (merged @ w_out[l])

        h = rmsnorm(x, g_ffn[l]) * (1.0 + sc2) + sh2
        hid = h @ w_up[l]
        gate = causal_dwconv(hid[..., :D_FF], w_ffconv[l])
        x = x + gt2 * ((jax.nn.silu(gate) * hid[..., D_FF:]) @ w_down[l])
    return x
```

```python
import contextlib
import numpy as np
import concourse.bass as bass
import concourse.mybir as mybir
from concourse.bass_utils import run_bass_kernel_spmd

F32 = mybir.dt.float32
BF16 = mybir.dt.bfloat16
AF = mybir.ActivationFunctionType
ALU = mybir.AluOpType
AX = mybir.AxisListType

D = 2048
DEPTH = 2
MIXW = 1024
DFF = 5632
NMOD = 6
EPS = 1e-6
NIN = 16392
ENGS = ("sp", "act", "dve", "pool", "pe")


class Buf:
    __slots__ = ("name", "t", "w", "r", "dkey")

    def __init__(self, name, t=None):
        self.name = name
        self.t = t
        self.w = []
        self.r = []
        self.dkey = None

    def __getitem__(self, k):
        return self.t[k]


class KB:
    def __init__(self, nc):
        self.nc = nc
        self.ins = {e: [] for e in ENGS}
        self.known = {e: {} for e in ENGS}
        self.dma_cnt = []
        self.need = set()
        self.free_keys = []
        self.pending_keys = []

    def _newkey(self, buf):
        if self.free_keys:
            buf.dkey = self.free_keys.pop()
        else:
            buf.dkey = len(self.dma_cnt)
            self.dma_cnt.append(0)

    def release(self, buf):
        if buf.dkey is not None:
            self.pending_keys.append(buf.dkey)
            buf.dkey = None

    def _deps(self, reads, writes):
        deps = []
        for b in reads:
            deps.extend(b.w)
        for b in writes:
            deps.extend(b.w)
            deps.extend(b.r)
        return deps

    def _waits(self, eng, deps):
        kn = self.known[eng]
        best = {}
        for k, v in deps:
            if eng == "pe" and k == "pe":
                continue
            if kn.get(k, 0) >= v:
                continue
            if best.get(k, 0) < v:
                best[k] = v
        for k, v in best.items():
            kn[k] = v
            if isinstance(k, str):
                self.need.add((k, v))
        return list(best.items())

    def _commit(self, tok, reads, writes):
        for b in reads:
            b.r.append(tok)
            if len(b.r) > 48:
                m = {}
                for k, v in b.r:
                    if m.get(k, 0) < v:
                        m[k] = v
                b.r = list(m.items())
        for b in writes:
            b.w = [tok]
            b.r = []

    def op(self, eng, fn, reads=(), writes=()):
        waits = self._waits(eng, self._deps(reads, writes))
        lst = self.ins[eng]
        idx = len(lst) + 1
        lst.append(("op", fn, waits, idx))
        self._commit((eng, idx), reads, writes)

    def dma(self, eng, out, in_, reads=(), writes=(), key=None, part=False, **kw):
        if key is None:
            key = writes[0] if writes else reads[0]
        if key.dkey is None:
            self._newkey(key)
        waits = self._waits(eng, self._deps(reads, () if part else writes))
        self.dma_cnt[key.dkey] += 16
        tok = (key.dkey, self.dma_cnt[key.dkey])
        lst = self.ins[eng]
        lst.append(("dma", (out, in_, kw), waits, len(lst) + 1, key.dkey))
        self._commit(tok, reads, writes)

    def coll(self, kind, op, groups, in_ap, out_ap, reads, writes, key):
        if key.dkey is None:
            self._newkey(key)
        waits = self._waits("pool", self._deps(reads, writes))
        self.dma_cnt[key.dkey] += 1
        tok = (key.dkey, self.dma_cnt[key.dkey])
        lst = self.ins["pool"]
        lst.append(("coll", (kind, op, groups, in_ap, out_ap), waits, len(lst) + 1, key.dkey))
        self._commit(tok, reads, writes)

    def wait_all(self, eng, bufs):
        deps = []
        for b in bufs:
            deps.extend(b.w)
            deps.extend(b.r)
        waits = self._waits(eng, deps)
        lst = self.ins[eng]
        lst.append(("wait", None, waits, len(lst) + 1))

    def barrier(self):
        deps = []
        for e in ENGS:
            n = 0
            for rec in self.ins[e]:
                if rec[0] == "op":
                    n = rec[3]
            if n:
                deps.append((e, n))
        for k, c in enumerate(self.dma_cnt):
            if c:
                deps.append((k, c))
        for e in ENGS:
            waits = self._waits(e, deps)
            lst = self.ins[e]
            lst.append(("wait", None, waits, len(lst) + 1))
        self.free_keys.extend(self.pending_keys)
        self.pending_keys = []

    def emit(self):
        nc = self.nc
        val = {}
        for e in ENGS:
            v = 0
            for rec in self.ins[e]:
                if (e, rec[3]) in self.need:
                    assert rec[0] == "op"
                    v += 1
                    val[(e, rec[3])] = v
        with contextlib.ExitStack() as st:
            esem = {e: st.enter_context(nc.semaphore("s_" + e)) for e in ENGS}
            dsem = [st.enter_context(nc.semaphore("d%d" % i)) for i in range(len(self.dma_cnt))]
            block = st.enter_context(nc.Block())

            def run(e, eng):
                for rec in self.ins[e]:
                    kind, fn, waits, idx = rec[0], rec[1], rec[2], rec[3]
                    for k, v in waits:
                        if isinstance(k, str):
                            eng.wait_ge(esem[k], val[(k, v)])
                        else:
                            eng.wait_ge(dsem[k], v)
                    if kind == "op":
                        i = fn(eng)
                        if (e, idx) in val:
                            i.then_inc(esem[e], 1)
                    elif kind == "dma":
                        out, in_, kw = fn
                        eng.dma_start(out=out, in_=in_, **kw).then_inc(dsem[rec[4]], 16)
                    elif kind == "coll":
                        ckind, cop, groups, in_ap, out_ap = fn
                        eng.collective_compute(ckind, cop, replica_groups=groups, ins=[in_ap],
                                               outs=[out_ap]).then_inc(dsem[rec[4]], 1)

            @block.sync
            def _(eng):
                run("sp", eng)

            @block.scalar
            def _(eng):
                run("act", eng)

            @block.vector
            def _(eng):
                run("dve", eng)

            @block.gpsimd
            def _(eng):
                run("pool", eng)

            @block.tensor
            def _(eng):
                run("pe", eng)


class Prog:
    def __init__(self, S, io=None, layers=(0, 1)):
        self.S = S
        self.NC = S // 128
        self.io = io or {}
        self.nc = bass.Bass("TRN2", target_bir_lowering=False)
        self.kb = KB(self.nc)
        self.drams = {}
        self.layers = layers
        self.uid = 0

    def dram(self, name, shape, dtype=F32, kind=None):
        if name in self.drams:
            return self.drams[name]
        if name[:2] in ("wA", "wI", "sp", "wc", "gm", "gf", "wG", "wB", "wO", "wU", "wD", "wf", "se", "cm", "wa", "ba"):
            kind = "in"
        kind = {"in": "ExternalInput", "out": "ExternalOutput"}.get(self.io.get(name, kind), "Internal")
        t = self.nc.dram_tensor(name, list(shape), dtype, kind=kind)
        b = Buf(name, t.ap())
        b.t = t.ap()
        self.drams[name] = b
        return b

    def sb(self, st, name, shape, dtype):
        self.uid += 1
        t = st.enter_context(self.nc.sbuf_tensor("%s_%d" % (name, self.uid), list(shape), dtype))
        b = Buf(name, t)
        st.callback(self.kb.release, b)
        return b

    def ps(self, st, name):
        self.uid += 1
        t = st.enter_context(self.nc.psum_tensor("%s_%d" % (name, self.uid), [128, 512], F32))
        return Buf(name, t)

    def mm(self, out, lhsT, rhs, start, stop, reads, writes):
        self.kb.op("pe", lambda e: e.matmul(out, lhsT=lhsT, rhs=rhs, start=start, stop=stop), reads, writes)

    def act(self, out, in_, func, reads, writes, eng="act", **kw):
        self.kb.op(eng, lambda e: e.activation(out=out, in_=in_, func=func, **kw), reads, writes)

    def tt(self, out, in0, in1, op, reads, writes, eng="dve"):
        self.kb.op(eng, lambda e: e.tensor_tensor(out=out, in0=in0, in1=in1, op=op), reads, writes)

    def ts(self, out, in0, s1, s2, op0, op1, reads, writes, eng="dve", **kw):
        if op1 is None:
            self.kb.op(eng, lambda e: e.tensor_scalar(out=out, in0=in0, scalar1=s1, scalar2=None, op0=op0, **kw), reads, writes)
        else:
            self.kb.op(eng, lambda e: e.tensor_scalar(out=out, in0=in0, scalar1=s1, scalar2=s2, op0=op0, op1=op1, **kw), reads, writes)

    def stt(self, out, in0, scalar, in1, op0, op1, reads, writes, eng="dve"):
        self.kb.op(eng, lambda e: e.scalar_tensor_tensor(out=out, in0=in0, scalar=scalar, in1=in1, op0=op0, op1=op1), reads, writes)

    def copy(self, out, in_, reads, writes, eng="dve"):
        if eng == "act":
            self.kb.op("act", lambda e: e.activation(out=out, in_=in_, func=AF.Copy), reads, writes)
        else:
            self.kb.op(eng, lambda e: e.tensor_copy(out=out, in_=in_), reads, writes)

    def memset(self, buf, ap, v, eng="dve"):
        self.kb.op(eng, lambda e: e.memset(ap, v), (), [buf])

    def rstd(self, out, ss, n, reads, writes):
        self.act(out, ss, AF.Sqrt, reads, writes, scale=1.0 / n, bias=EPS)
        self.kb.op("dve", lambda e: e.reciprocal(out=out, in_=out), writes, writes)

    def consts(self, st):
        self.io.setdefault("cst", "in")
        c = self.dram("cst", [128, 5 * 128], F32)
        self.ident = self.sb(st, "ident", [128, 128], BF16)
        self.antid = self.sb(st, "antid", [128, 128], BF16)
        self.m_le = self.sb(st, "m_le", [128, 128], F32)
        self.m_gt = self.sb(st, "m_gt", [128, 128], F32)
        self.tri32 = self.sb(st, "tri32", [128, 128], F32)
        self.ones32 = self.sb(st, "ones32", [128, 128], F32)
        self.identf = self.sb(st, "identf", [128, 128], F32)
        k = self.kb
        k.dma("pool", self.ident[:, :], c[:, 0:128], reads=[c], writes=[self.ident])
        k.dma("pool", self.antid[:, :], c[:, 128:256], reads=[c], writes=[self.antid])
        k.dma("sp", self.m_le[:, :], c[:, 256:384], reads=[c], writes=[self.m_le])
        k.dma("sp", self.m_gt[:, :], c[:, 384:512], reads=[c], writes=[self.m_gt])
        k.dma("sp", self.tri32[:, :], c[:, 256:384], reads=[c], writes=[self.tri32])
        k.dma("sp", self.identf[:, :], c[:, 0:128], reads=[c], writes=[self.identf])
        self.blk32 = self.sb(st, "blk32", [128, 128], F32)
        k.dma("sp", self.blk32[:, :], c[:, 512:640], reads=[c], writes=[self.blk32])
        self.memset(self.ones32, self.ones32[:, :], 1.0)
        self.pb = [self.ps(st, "pb%d" % i) for i in range(8)]
        k.barrier()


def host_consts():
    c = np.zeros((128, 5 * 128), np.float32)
    p = np.arange(128)[:, None]
    j = np.arange(128)[None, :]
    c[:, 0:128] = (p == j)
    c[:, 128:256] = (p + j == 127)
    c[:, 256:384] = (p <= j)
    c[:, 384:512] = (j > p)
    c[:, 512:640] = ((p // 64) == (j // 64))
    return c


def load_bcast(P, dst, row_ap, reads, eng="sp"):
    P.kb.dma(eng, dst[:, :], row_ap.partition_broadcast(128), reads=reads, writes=[dst])


def make_G_SH(P, st, modb, gvec, l, which):
    G = P.sb(st, "G", [128, D], F32)
    SH = P.sb(st, "SH", [128, D], F32)
    with contextlib.ExitStack() as s2:
        tg = P.sb(s2, "tg", [128, D], F32)
        load_bcast(P, SH, modb.t[3 * which + 0, :], [modb])
        load_bcast(P, G, modb.t[3 * which + 1, :], [modb])
        load_bcast(P, tg, gvec.t[l, :], [gvec])
        P.stt(G[:, :], G[:, :], 1.0, tg[:, :], ALU.add, ALU.mult, [G, tg], [G])
        P.kb.barrier()
    return G, SH


def norm_pass(P, st, blocks, hT, hTb_of, G, SH, pbanks):
    with contextlib.ExitStack() as s2:
        xt = [P.sb(s2, "xt%d" % i, [128, D], F32) for i in range(2)]
        tmp = [P.sb(s2, "ntmp%d" % i, [128, D], F32) for i in range(2)]
        yt = [P.sb(s2, "yt%d" % i, [128, D], BF16) for i in range(2)]
        ss = [P.sb(s2, "ss%d" % i, [128, 4], F32) for i in range(2)]
        def stats(bi):
            xap, xbuf, n, dcol, rev = blocks[bi]
            i = bi % 2
            P.kb.dma("sp", xt[i][0:n, :], xap, reads=[xbuf], writes=[xt[i]])
            P.memset(ss[i], ss[i][:, :], 0.0)
            P.act(tmp[i][0:n, :], xt[i][0:n, :], AF.Square, [xt[i]], [tmp[i], ss[i]], accum_out=ss[i][0:n, 0:1])
            P.act(ss[i][0:n, 2:3], ss[i][0:n, 0:1], AF.Sqrt, [ss[i]], [ss[i]], scale=1.0 / D, bias=EPS)

        def apply(bi):
            xap, xbuf, n, dcol, rev = blocks[bi]
            i = bi % 2
            P.kb.op("dve", lambda e, i=i, n=n: e.reciprocal(out=ss[i][0:n, 2:3], in_=ss[i][0:n, 2:3]), [ss[i]], [ss[i]])
            P.stt(tmp[i][0:n, :], xt[i][0:n, :], ss[i][0:n, 2:3], G[0:n, :], ALU.mult, ALU.mult, [xt[i], ss[i], G], [tmp[i]])
            P.tt(yt[i][0:n, :], tmp[i][0:n, :], SH[0:n, :], ALU.add, [tmp[i], SH], [yt[i]])
            idm = P.antid if rev else P.ident
            for g in range(4):
                pb = pbanks[(bi * 4 + g) % len(pbanks)]
                for j in range(4):
                    c = g * 4 + j
                    P.mm(pb[:, j * 128:j * 128 + n], yt[i][0:n, c * 128:(c + 1) * 128], idm[0:n, 0:n] if not rev else idm[0:n, 128 - n:128],
                         True, True, [yt[i], idm], [pb])
                src = pb[:, :].rearrange("p (a b) -> p a b", b=128)[:, :, 0:n]
                dst = hT[:, g * 4:(g + 1) * 4, dcol:dcol + n]
                P.copy(dst, src, [pb], [hTb_of(dcol)], eng="act")

        stats(0)
        for bi in range(len(blocks)):
            if bi + 1 < len(blocks):
                stats(bi + 1)
            apply(bi)


def load_w(P, wbuf, wdram, c0, ncols, kch, first_eng="pool"):
    src = wdram.t.rearrange("(kc p) n -> p kc n", p=128)[:, 0:kch, c0:c0 + ncols]
    P.kb.dma("pool", wbuf[:, 0:kch, 0:ncols], src, reads=[wdram], writes=[wbuf])


SP_BIF, SP_GMOUT, SP_GDQ, SP_GDK, SP_LQ1, SP_LK1, SP_LQ2, SP_LK2, SP_GDSUB, SP_N = 0, 4, 516, 580, 644, 708, 772, 836, 900, 1028


def stageA_proj(P, l, xf, modb, gmix):
    S, NC, kb = P.S, P.NC, P.kb
    TT = min(512, S)
    NT = S // TT
    wA = P.dram("wA%d" % l, [D, 5120])
    wIF = P.dram("wIF%d" % l, [D, 4])
    spk = P.dram("sp%d" % l, [SP_N])
    wcv = P.dram("wcv%d" % l, [1024, 4])
    qkT = P.dram("qkT", [1024, S], BF16)
    mv = P.dram("mv", [S, 512], BF16)
    mos = P.dram("mos", [S, 512], BF16)
    ifg = P.dram("ifg", [S, 4], F32)
    dqk = P.dram("dqkT", [1024, S], BF16)
    dv = P.dram("dv", [S, 512], BF16)
    sqk = P.dram("sqkT", [1024, S], BF16)
    sv = P.dram("sv", [S, 512], BF16)
    pb = P.pb
    with contextlib.ExitStack() as st:
        hT = P.sb(st, "hT", [128, 16, S], BF16)
        hTb = [Buf("hTb%d" % i) for i in range(NT)]
        hTb_of = lambda col: hTb[col // TT]
        wif = P.sb(st, "wif", [128, 16, 4], BF16)
        wcvt = P.sb(st, "wcvt", [128, 8, 4], F32)
        gq = P.sb(st, "gq", [128, 2], F32)
        kb.dma("sp", wcvt[:, :, :], wcv.t.rearrange("(c p) j -> p c j", p=128), reads=[wcv], writes=[wcvt])
        for half in range(2):
            kb.dma("sp", gq[half * 64:(half + 1) * 64, 0:1], spk.t[SP_GDQ:SP_GDQ + 64].rearrange("(p o) -> p o", o=1), reads=[spk], writes=[gq], part=half > 0)
            kb.dma("sp", gq[half * 64:(half + 1) * 64, 1:2], spk.t[SP_GDK:SP_GDK + 64].rearrange("(p o) -> p o", o=1), reads=[spk], writes=[gq], part=True)
        P.ts(gq[:, 0:1], gq[:, 0:1], 0.125, None, ALU.mult, None, [gq], [gq])
        load_w(P, wif, wIF, 0, 4, 16)
        for rev in (False, True):
            with contextlib.ExitStack() as s2:
                G, SH = make_G_SH(P, s2, modb, gmix, l, 0)
                blocks = []
                for tb in range(NC):
                    dcol = (NC - 1 - tb) * 128 if rev else tb * 128
                    blocks.append((xf.rows(tb * 128, 128), xf.buf, 128, dcol, rev))
                norm_pass(P, s2, blocks, hT, hTb_of, G, SH, pb[0:4])
                kb.barrier()
            groups = ([("conv", 0, qkT, 0), ("conv", 1, qkT, 512), ("tm", 2, mv, None), ("sig", 3, mos, None),
                       ("qkn", 4, dqk, 0), ("qkn", 5, dqk, 512), ("tm", 6, dv, None), ("if", None, None, None)]
                      if not rev else
                      [("scl", 7, sqk, 0), ("scl", 8, sqk, 512), ("tm", 9, sv, None)])
            with contextlib.ExitStack() as s2:
                wb = [P.sb(s2, "wb%d" % i, [128, 16, 512], BF16) for i in range(2)]
                ZB = P.sb(s2, "ZB", [128, S + 3], F32)
                CH = max(S // 2, 128)
                CA = P.sb(s2, "CA", [128, CH], F32)
                QO = [P.sb(s2, "QO%d" % i, [128, S], BF16) for i in range(1)]
                SQ = [P.sb(s2, "SQ%d" % i, [128, TT], F32) for i in range(1)]
                RS = [P.sb(s2, "RS%d" % i, [128, TT], F32) for i in range(1)]
                stg = [P.sb(s2, "stg%d" % i, [128, 512], BF16) for i in range(2)]
                ifs = [P.sb(s2, "ifs%d" % i, [128, 4], F32) for i in range(2)]
                P.memset(ZB, ZB[:, 0:3], 0.0)
                pi = 0
                qi = 0
                for gi, (kind, g, dst, roff) in enumerate(groups):
                    if kind == "if":
                        for tb in range(NC):
                            pbk = pb[pi % 4]; pi += 1
                            for kc in range(16):
                                P.mm(pbk[:, 0:4], hT[:, kc, tb * 128:(tb + 1) * 128], wif[:, kc, :], kc == 0, kc == 15,
                                     [hTb_of(tb * 128), wif], [pbk])
                            s = ifs[tb % 2]
                            P.copy(s[:, :], pbk[:, 0:4], [pbk], [s])
                            kb.dma("sp", ifg.t[tb * 128:(tb + 1) * 128, :], s[:, :], reads=[s], writes=[ifg], key=s)
                        continue
                    w = wb[gi % 2]
                    load_w(P, w, wA, g * 512, 512, 16)
                    if kind in ("tm", "sig"):
                        for tb in range(NC):
                            pbk = pb[pi % 4]; pi += 1
                            for kc in range(16):
                                P.mm(pbk[:, :], hT[:, kc, tb * 128:(tb + 1) * 128], w[:, kc, :], kc == 0, kc == 15,
                                     [hTb_of(tb * 128), w], [pbk])
                            s = stg[tb % 2]
                            if kind == "sig":
                                P.act(s[:, :], pbk[:, :], AF.Sigmoid, [pbk], [s])
                            else:
                                P.copy(s[:, :], pbk[:, :], [pbk], [s], eng="act" if tb % 2 else "dve")
                            drow = (NC - 1 - tb) if False else tb
                            kb.dma("sp", dst.t[drow * 128:(drow + 1) * 128, :], s[:, :], reads=[s], writes=[dst], key=s)
                        continue
                    for cb in range(4):
                        qo = QO[0]; qi += 1
                        for tt in range(NT):
                            pbk = pb[pi % 4]; pi += 1
                            for kc in range(16):
                                P.mm(pbk[:, 0:TT], w[:, kc, cb * 128:(cb + 1) * 128], hT[:, kc, tt * TT:(tt + 1) * TT], kc == 0, kc == 15,
                                     [w, hTb[tt]], [pbk])
                            if kind == "conv":
                                P.copy(ZB[:, 3 + tt * TT:3 + (tt + 1) * TT], pbk[:, 0:TT], [pbk], [ZB], eng="act")
                            elif kind == "scl":
                                P.act(qo[:, tt * TT:(tt + 1) * TT], pbk[:, 0:TT], AF.Copy, [pbk], [qo],
                                      scale=(128.0 ** -0.5) if g == 7 else 1.0)
                            else:
                                sq = SQ[0]; rs = RS[0]
                                P.act(sq[:, 0:TT], pbk[:, 0:TT], AF.Square, [pbk], [sq])
                                p2 = pb[4 + (pi % 2)]
                                P.mm(p2[:, 0:TT], P.blk32[:, :], sq[:, 0:TT], True, True, [P.blk32, sq], [p2])
                                P.rstd(rs[:, 0:TT], p2[:, 0:TT], 64, [p2], [rs])
                                gcol = gq[:, 0:1] if g == 4 else gq[:, 1:2]
                                P.stt(qo[:, tt * TT:(tt + 1) * TT], pbk[:, 0:TT], gcol, rs[:, 0:TT], ALU.mult, ALU.mult,
                                      [pbk, gq, rs], [qo])
                        if kind == "conv":
                            ch = g * 4 + cb
                            for hv in range(S // CH):
                                o = hv * CH
                                P.ts(CA[:, :], ZB[:, o:o + CH], wcvt[:, ch, 0:1], None, ALU.mult, None, [ZB, wcvt], [CA])
                                for j in range(1, 4):
                                    P.stt(CA[:, :], ZB[:, o + j:o + CH + j], wcvt[:, ch, j:j + 1], CA[:, :], ALU.mult, ALU.add, [ZB, wcvt, CA], [CA])
                                P.act(qo[:, o:o + CH], CA[:, :], AF.Silu, [CA], [qo])
                        r0 = roff + cb * 128
                        kb.dma("sp", dst.t[r0:r0 + 128, :], qo[:, :], reads=[qo], writes=[dst], key=qo)
                kb.barrier()


def stageA_mlstm(P, l, obT):
    S, NC, kb = P.S, P.NC, P.kb
    pb = P.pb
    spk = P.dram("sp%d" % l, [SP_N])
    qkT = P.dram("qkT", [1024, S], BF16)
    mv = P.dram("mv", [S, 512], BF16)
    mos = P.dram("mos", [S, 512], BF16)
    ifg = P.dram("ifg", [S, 4], F32)
    with contextlib.ExitStack() as st:
        IFt = P.sb(st, "IFt", [128, NC, 4], F32)
        LF = P.sb(st, "LF", [128, NC, 2], F32)
        KS = P.sb(st, "KS", [128, NC, 2], F32)
        QS = P.sb(st, "QS", [128, NC, 2], F32)
        CDc = P.sb(st, "CDc", [128, NC, 2], F32)
        bifb = P.sb(st, "bifb", [128, 4], F32)
        gmo = P.sb(st, "gmo", [128, 512], F32)
        kb.dma("sp", IFt[:, :, :], ifg.t.rearrange("(c p) f -> p c f", p=128), reads=[ifg], writes=[IFt])
        load_bcast(P, bifb, spk.t[SP_BIF:SP_BIF + 4], [spk])
        load_bcast(P, gmo, spk.t[SP_GMOUT:SP_GMOUT + 512], [spk])
        for j in range(4):
            P.ts(IFt[:, :, j], IFt[:, :, j], bifb[:, j:j + 1], None, ALU.add, None, [IFt, bifb], [IFt])
        P.act(LF[:, :, :], IFt[:, :, 2:4], AF.Exp, [IFt], [LF], scale=-1.0)
        P.act(LF[:, :, :], LF[:, :, :], AF.Ln, [LF], [LF], bias=1.0)
        P.ts(LF[:, :, :], LF[:, :, :], -1.0, None, ALU.mult, None, [LF], [LF])
        lf2 = LF[:, :, :].rearrange("p c h -> p (c h)")
        P.mm(pb[0][:, 0:NC * 2], P.tri32[:, :], lf2, True, True, [P.tri32, LF], [pb[0]])
        P.mm(pb[1][:, 0:NC * 2], P.ones32[:, :], lf2, True, True, [P.ones32, LF], [pb[1]])
        Bv = pb[0][:, 0:NC * 2].rearrange("p (c h) -> p c h", h=2)
        BLv = pb[1][:, 0:NC * 2].rearrange("p (c h) -> p c h", h=2)
        P.tt(KS[:, :, :], IFt[:, :, 0:2], Bv, ALU.subtract, [IFt, pb[0]], [KS])
        P.act(KS[:, :, :], KS[:, :, :], AF.Exp, [KS], [KS])
        P.act(QS[:, :, :], Bv, AF.Exp, [pb[0]], [QS])
        P.ts(QS[:, :, :], QS[:, :, :], 1.0 / 16, None, ALU.mult, None, [QS], [QS])
        P.act(CDc[:, :, :], BLv, AF.Exp, [pb[1]], [CDc])
        kb.barrier()
        for h in range(2):
            with contextlib.ExitStack() as s2:
                qT = P.sb(s2, "qT", [128, 2, S], BF16)
                kT = P.sb(s2, "kT", [128, 2, S], BF16)
                Vp = P.sb(s2, "Vp", [128, NC, 257], BF16)
                VS = P.sb(s2, "VS", [128, NC, 257], BF16)
                MO = P.sb(s2, "MO", [128, NC, 256], BF16)
                ktm = P.sb(s2, "ktm", [128, NC, 256], BF16)
                obm = P.sb(s2, "obm", [128, 2, S], BF16)
                Cs = P.sb(s2, "Cs", [128, 2, 257], F32)
                Cb = P.sb(s2, "Cb", [128, 2, 257], BF16)
                wT = [P.sb(s2, "wT%d" % i, [128, 128], BF16) for i in range(2)]
                hm = [P.sb(s2, "hm%d" % i, [128, 256], F32) for i in range(2)]
                tm2 = [P.sb(s2, "tm2%d" % i, [128, 256], F32) for i in range(2)]
                om = [P.sb(s2, "om%d" % i, [128, 256], BF16) for i in range(2)]
                sc = [P.sb(s2, "sc%d" % i, [128, 8], F32) for i in range(2)]
                kb.dma("sp", qT[:, :, :], qkT.t[h * 256:(h + 1) * 256, :].rearrange("(dc p) s -> p dc s", p=128), reads=[qkT], writes=[qT])
                kb.dma("sp", kT[:, :, :], qkT.t[512 + h * 256:512 + (h + 1) * 256, :].rearrange("(dc p) s -> p dc s", p=128), reads=[qkT], writes=[kT])
                kb.dma("sp", Vp[:, :, 0:256], mv.t[:, h * 256:(h + 1) * 256].rearrange("(c p) d -> p c d", p=128), reads=[mv], writes=[Vp])
                kb.dma("sp", MO[:, :, :], mos.t[:, h * 256:(h + 1) * 256].rearrange("(c p) d -> p c d", p=128), reads=[mos], writes=[MO])
                P.memset(Vp, Vp[:, :, 256:257], 1.0, eng="pool")
                P.memset(Cs, Cs[:, :, :], 0.0)
                for c in range(NC):
                    pk = pb[2 + c % 2]
                    for dc in range(2):
                        P.mm(pk[:, dc * 128:(dc + 1) * 128], kT[:, dc, c * 128:(c + 1) * 128], P.ident[:, :], True, True, [kT, P.ident], [pk])
                    P.copy(ktm[:, c, :], pk[:, 0:256], [pk], [ktm], eng="act" if c % 2 else "dve")
                    P.act(VS[:, c, :], Vp[:, c, :], AF.Copy, [Vp, KS], [VS], scale=KS[:, c, h:h + 1])
                for c in range(NC):
                    cs = slice(c * 128, (c + 1) * 128)
                    i = c % 2
                    pS, pN, pD0, pD1, pT = pb[0], pb[1], pb[4], pb[5], pb[6 + i]
                    for dc in range(2):
                        P.mm(pS[:, 0:128], kT[:, dc, cs], qT[:, dc, cs], dc == 0, dc == 1, [kT, qT], [pS])
                    P.stt(wT[i][:, :], pS[:, 0:128], KS[:, c, h:h + 1], P.m_le[:, :], ALU.mult, ALU.mult, [pS, KS, P.m_le], [wT[i]])
                    P.mm(pN[:, 0:257], wT[i][:, :], Vp[:, c, :], True, c == 0, [wT[i], Vp], [pN])
                    if c > 0:
                        for dc in range(2):
                            P.mm(pN[:, 0:257], qT[:, dc, cs], Cb[:, dc, :], False, dc == 1, [qT, Cb], [pN])
                    s_ = sc[i]
                    P.tt(s_[:, 0:1], pN[:, 256:257], QS[:, c, h:h + 1], ALU.mult, [pN, QS], [s_])
                    P.act(s_[:, 1:2], s_[:, 0:1], AF.Abs, [s_], [s_])
                    P.ts(s_[:, 1:2], s_[:, 1:2], 1.0, None, ALU.max, None, [s_], [s_])
                    kb.op("dve", lambda e, s_=s_: e.reciprocal(out=s_[:, 2:3], in_=s_[:, 1:2]), [s_], [s_])
                    P.tt(s_[:, 3:4], s_[:, 2:3], QS[:, c, h:h + 1], ALU.mult, [s_, QS], [s_])
                    P.ts(hm[i][:, :], pN[:, 0:256], s_[:, 3:4], None, ALU.mult, None, [pN, s_], [hm[i]])
                    for dc, pD in ((0, pD0), (1, pD1)):
                        P.mm(pD[:, 0:257], ktm[:, c, dc * 128:(dc + 1) * 128], VS[:, c, :], True, True, [ktm, VS], [pD])
                        P.tt(Cs[:, dc, :], Cs[:, dc, :], pD[:, 0:257], ALU.add, [Cs, pD], [Cs])
                    P.ts(Cs[:, :, :], Cs[:, :, :], CDc[:, c, h:h + 1], None, ALU.mult, None, [Cs, CDc], [Cs])
                    P.copy(Cb[:, :, :], Cs[:, :, :], [Cs], [Cb], eng="act")
                    P.memset(s_, s_[:, 4:5], 0.0)
                    P.act(tm2[i][:, :], hm[i][:, :], AF.Square, [hm[i]], [tm2[i], s_], accum_out=s_[:, 4:5])
                    P.rstd(s_[:, 6:7], s_[:, 4:5], 256, [s_], [s_])
                    P.stt(tm2[i][:, :], hm[i][:, :], s_[:, 6:7], gmo[:, h * 256:(h + 1) * 256], ALU.mult, ALU.mult, [hm[i], s_, gmo], [tm2[i]])
                    P.tt(om[i][:, :], tm2[i][:, :], MO[:, c, :], ALU.mult, [tm2[i], MO], [om[i]])
                    for dc in range(2):
                        P.mm(pT[:, dc * 128:(dc + 1) * 128], om[i][:, dc * 128:(dc + 1) * 128], P.ident[:, :], True, True, [om[i], P.ident], [pT])
                    P.copy(obm[:, :, cs], pT[:, 0:256].rearrange("p (a b) -> p a b", b=128), [pT], [obm], eng="act")
                kb.dma("sp", obT.t[h * 256:(h + 1) * 256, :].rearrange("(dc p) s -> p dc s", p=128), obm[:, :, :], reads=[obm], writes=[obT], key=obm)
                kb.barrier()


def stageA_diff(P, l, obT):
    import math
    S, NC, kb = P.S, P.NC, P.kb
    pb = P.pb
    TT = min(512, S)
    NP = S // TT
    nqb = TT // 128
    lam_init = 0.8 - 0.6 * math.exp(-0.3 * l)
    spk = P.dram("sp%d" % l, [SP_N])
    dqk = P.dram("dqkT", [1024, S], BF16)
    dv = P.dram("dv", [S, 512], BF16)
    with contextlib.ExitStack() as st:
        lt = [P.sb(st, "lt%d" % i, [128, 64], F32) for i in range(4)]
        lam = P.sb(st, "lam", [128, 8], F32)
        gds = P.sb(st, "gds", [128, 128], F32)
        for i, off in enumerate((SP_LQ1, SP_LK1, SP_LQ2, SP_LK2)):
            load_bcast(P, lt[i], spk.t[off:off + 64], [spk])
        load_bcast(P, gds, spk.t[SP_GDSUB:SP_GDSUB + 128], [spk])
        P.ts(gds[:, :], gds[:, :], 1.0 - lam_init, None, ALU.mult, None, [gds], [gds])
        P.tt(lt[0][:, :], lt[0][:, :], lt[1][:, :], ALU.mult, [lt[0], lt[1]], [lt[0]])
        P.tt(lt[2][:, :], lt[2][:, :], lt[3][:, :], ALU.mult, [lt[2], lt[3]], [lt[2]])
        kb.op("dve", lambda e: e.reduce_sum(out=lam[:, 0:1], in_=lt[0][:, :], axis=AX.X), [lt[0]], [lam])
        kb.op("dve", lambda e: e.reduce_sum(out=lam[:, 1:2], in_=lt[2][:, :], axis=AX.X), [lt[2]], [lam])
        P.act(lam[:, 2:4], lam[:, 0:2], AF.Exp, [lam], [lam])
        P.tt(lam[:, 4:5], lam[:, 3:4], lam[:, 2:3], ALU.subtract, [lam], [lam])
        P.ts(lam[:, 5:6], lam[:, 4:5], -lam_init, None, ALU.add, None, [lam], [lam])
        kb.barrier()
        for h in range(4):
            with contextlib.ExitStack() as s2:
                qT = P.sb(s2, "dqT", [128, S], BF16)
                kT = P.sb(s2, "dkT", [128, S], BF16)
                Vp = P.sb(s2, "dVp", [128, NC, 129], BF16)
                obd = P.sb(s2, "obd", [128, S], BF16)
                Eb = [P.sb(s2, "Eb%d" % i, [128, TT], BF16) for i in range(3)]
                O = [P.sb(s2, "O%d" % i, [128, nqb, 128], F32) for i in range(2)]
                rc = [P.sb(s2, "rc%d" % i, [128, 8], F32) for i in range(2)]
                od = [P.sb(s2, "od%d" % i, [128, 128], F32) for i in range(2)]
                t2 = [P.sb(s2, "t2%d" % i, [128, 128], F32) for i in range(2)]
                ob = [P.sb(s2, "ob%d" % i, [128, 128], BF16) for i in range(2)]
                kb.dma("sp", qT[:, :], dqk.t[h * 128:(h + 1) * 128, :], reads=[dqk], writes=[qT])
                kb.dma("sp", kT[:, :], dqk.t[512 + h * 128:512 + (h + 1) * 128, :], reads=[dqk], writes=[kT])
                kb.dma("sp", Vp[:, :, 0:128], dv.t[:, h * 128:(h + 1) * 128].rearrange("(c p) d -> p c d", p=128), reads=[dv], writes=[Vp])
                P.memset(Vp, Vp[:, :, 128:129], 1.0, eng="pool")
                ei = 0
                pending = [None]
                for qp in range(NP):
                    for c in range(2):
                        ps_ = slice(c * 64, (c + 1) * 64)
                        nk = (qp + 1) * nqb
                        def front(kbk):
                            q0 = max(0, kbk - qp * nqb)
                            w = TT - q0 * 128
                            qc0 = qp * TT + q0 * 128
                            pS = (pb[0], pb[1], pb[6])[kbk % 3]
                            P.mm(pS[:, 0:w], kT[ps_, kbk * 128:(kbk + 1) * 128], qT[ps_, qc0:qc0 + w], True, True, [kT, qT], [pS])
                            E = Eb[kbk % 3]
                            P.act(E[:, 0:w], pS[:, 0:w], AF.Exp, [pS], [E])
                            if kbk >= qp * nqb:
                                P.tt(E[:, 0:128], E[:, 0:128], P.m_le[:, :], ALU.mult, [E, P.m_le], [E], eng="pool")

                        def back(kbk):
                            q0 = max(0, kbk - qp * nqb)
                            E = Eb[kbk % 3]
                            for qb in range(q0, nqb):
                                pa = pb[2 + qb]
                                P.mm(pa[:, 0:129], E[:, (qb - q0) * 128:(qb - q0 + 1) * 128], Vp[:, kbk, :], kbk == 0, kbk == qp * nqb + qb, [E, Vp], [pa])

                        front(0)
                        if nk > 1:
                            front(1)
                        if c == 0 and pending[0] is not None:
                            pending[0]()
                            pending[0] = None
                        for kbk in range(nk):
                            if kbk + 2 < nk:
                                front(kbk + 2)
                            back(kbk)
                        for qb in range(nqb):
                            pa = pb[2 + qb]
                            r = rc[qb % 2]
                            kb.op("dve", lambda e, r=r, pa=pa: e.reciprocal(out=r[:, 0:1], in_=pa[:, 128:129]), [pa], [r])
                            P.ts(O[c][:, qb, :], pa[:, 0:128], r[:, 0:1], None, ALU.mult, None, [pa, r], [O[c]])
                    def tail(qp=qp):
                      for qb in range(nqb):
                        i = qb % 2
                        r = rc[i]
                        P.stt(od[i][:, :], O[1][:, qb, :], lam[:, 5:6], O[0][:, qb, :], ALU.mult, ALU.add, [O[0], O[1], lam], [od[i]])
                        P.memset(r, r[:, 1:2], 0.0)
                        P.act(t2[i][:, :], od[i][:, :], AF.Square, [od[i]], [t2[i], r], accum_out=r[:, 1:2])
                        P.rstd(r[:, 3:4], r[:, 1:2], 128, [r], [r])
                        P.stt(ob[i][:, :], od[i][:, :], r[:, 3:4], gds[:, :], ALU.mult, ALU.mult, [od[i], r, gds], [ob[i]])
                        pT = pb[7]
                        P.mm(pT[:, 0:128], ob[i][:, :], P.ident[:, :], True, True, [ob[i], P.ident], [pT])
                        c0 = qp * TT + qb * 128
                        P.copy(obd[:, c0:c0 + 128], pT[:, 0:128], [pT], [obd], eng="act")
                    pending[0] = tail
                pending[0]()
                pending[0] = None
                kb.dma("sp", obT.t[512 + h * 128:512 + (h + 1) * 128, :], obd[:, :], reads=[obd], writes=[obT], key=obd)
                kb.barrier()


def stageA_sb(P, l, obT):
    S, NC, kb = P.S, P.NC, P.kb
    pb = P.pb
    sqk = P.dram("sqkT", [1024, S], BF16)
    sv = P.dram("sv", [S, 512], BF16)
    with contextlib.ExitStack() as st:
        ones = P.sb(st, "sones", [128, 512], F32)
        P.memset(ones, ones[:, :], 1.0)
        for h in range(4):
            with contextlib.ExitStack() as s2:
                qT = P.sb(s2, "sqT", [128, S], BF16)
                kT = P.sb(s2, "skT", [128, S], BF16)
                V = P.sb(s2, "sV", [128, NC, 128], BF16)
                obs = P.sb(s2, "obs", [128, S], BF16)
                Lb = [P.sb(s2, "Lb%d" % i, [128, 512], F32) for i in range(3)]
                CSb = [P.sb(s2, "CSb%d" % i, [128, 512], F32) for i in range(2)]
                Wb = [P.sb(s2, "Wb%d" % i, [128, 512], F32) for i in range(2)]
                Ab = [P.sb(s2, "Ab%d" % i, [128, 512], BF16) for i in range(2)]
                ATb = [P.sb(s2, "ATb%d" % i, [128, 512], BF16) for i in range(2)]
                osb = [P.sb(s2, "osb%d" % i, [128, 128], BF16) for i in range(2)]
                kb.dma("sp", qT[:, :], sqk.t[h * 128:(h + 1) * 128, :], reads=[sqk], writes=[qT])
                kb.dma("sp", kT[:, :], sqk.t[512 + h * 128:512 + (h + 1) * 128, :], reads=[sqk], writes=[kT])
                kb.dma("sp", V[:, :, :], sv.t[:, h * 128:(h + 1) * 128].rearrange("(c p) d -> p c d", p=128), reads=[sv], writes=[V])
                pieces = []
                for rb in range(NC):
                    k0 = rb * 128
                    first = True
                    while k0 < S:
                        w = min(512, S - k0)
                        pieces.append((rb, k0, w, first, k0 + w >= S))
                        first = False
                        k0 += w
                state = {"carry": None}

                def front(n):
                    rb, k0, w, first, last = pieces[n]
                    L_ = Lb[n % 3]
                    pz = pb[n % 3]
                    P.mm(pz[:, 0:w], qT[:, rb * 128:(rb + 1) * 128], kT[:, k0:k0 + w], True, True, [qT, kT], [pz])
                    P.act(L_[:, 0:w], pz[:, 0:w], AF.Exp, [pz], [L_])
                    P.act(L_[:, 0:w], L_[:, 0:w], AF.Ln, [L_], [L_], bias=1.0)
                    if first:
                        P.tt(L_[:, 0:128], L_[:, 0:128], P.m_gt[:, :], ALU.mult, [L_, P.m_gt], [L_], eng="pool")

                def back(n):
                    rb, k0, w, first, last = pieces[n]
                    i = n % 2
                    pz = pb[n % 3]
                    L_ = Lb[n % 3]
                    po = pb[5 + rb % 2]
                    carry = None if first else state["carry"]
                    rd = [ones, L_] + ([] if carry is None else [carry[1]])
                    init = 0.0 if carry is None else carry[0]
                    kb.op("dve", lambda e, i=i, w=w, init=init, L_=L_: e.tensor_tensor_scan(
                        out=CSb[i][:, 0:w], data0=ones[:, 0:w], data1=L_[:, 0:w], initial=init, op0=ALU.mult, op1=ALU.add), rd, [CSb[i]])
                    state["carry"] = (CSb[i][:, w - 1:w], CSb[i])
                    P.tt(Wb[i][:, 0:w], pz[:, 0:w], CSb[i][:, 0:w], ALU.subtract, [pz, CSb[i]], [Wb[i]])
                    P.act(Ab[i][:, 0:w], Wb[i][:, 0:w], AF.Exp, [Wb[i]], [Ab[i]])
                    if first:
                        P.tt(Ab[i][:, 0:128], Ab[i][:, 0:128], P.m_gt[:, :], ALU.mult, [Ab[i], P.m_gt], [Ab[i]], eng="pool")

                def back2(n):
                    rb, k0, w, first, last = pieces[n]
                    i = n % 2
                    po = pb[5 + rb % 2]
                    pT = pb[3 + i]
                    nb = w // 128
                    for j in range(nb):
                        P.mm(pT[:, j * 128:(j + 1) * 128], Ab[i][:, j * 128:(j + 1) * 128], P.ident[:, :], True, True, [Ab[i], P.ident], [pT])
                    P.copy(ATb[i][:, 0:w], pT[:, 0:w], [pT], [ATb[i]], eng="dve")
                    for j in range(nb):
                        kblk = k0 // 128 + j
                        P.mm(po[:, 0:128], ATb[i][:, j * 128:(j + 1) * 128], V[:, kblk, :], first and j == 0, last and j == nb - 1, [ATb[i], V], [po])
                    if last:
                        o = osb[rb % 2]
                        P.copy(o[:, :], po[:, 0:128], [po], [o], eng="dve")
                        pT2 = pb[7]
                        P.mm(pT2[:, 0:128], o[:, :], P.antid[:, :], True, True, [o, P.antid], [pT2])
                        c0 = (NC - 1 - rb) * 128
                        P.copy(obs[:, c0:c0 + 128], pT2[:, 0:128], [pT2], [obs], eng="dve")

                NPc = len(pieces)
                front(0)
                if NPc > 1:
                    front(1)
                back(0)
                for n in range(NPc):
                    if n + 2 < NPc:
                        front(n + 2)
                    if n + 1 < NPc:
                        back(n + 1)
                    back2(n)
                kb.dma("sp", obT.t[1024 + h * 128:1024 + (h + 1) * 128, :], obs[:, :], reads=[obs], writes=[obT], key=obs)
                kb.barrier()


def stageA(P, l, xf, modb, gmix, obT):
    stageA_proj(P, l, xf, modb, gmix)
    stageA_mlstm(P, l, obT)
    stageA_diff(P, l, obT)
    stageA_sb(P, l, obT)


OFF = dict(mq=0, mk=1024, mv=2048, mo=3072, mi=4096, mf=4100, dq=4104, dk=5128, dv=6152, sq=7176, sk=8200, sv=9224,
           gm=10248, gd=12296, gs=14344)


def prep_A(inp, l, hh):
    w = inp["w_in"][l]
    sl = lambda o: w[:, o + hh * 512:o + (hh + 1) * 512]
    wA = np.concatenate([sl(OFF[k]) for k in ("mq", "mk", "mv", "mo", "dq", "dk", "dv", "sq", "sk", "sv")], axis=1)
    wIF = np.concatenate([w[:, 4096 + hh * 2:4096 + hh * 2 + 2], w[:, 4100 + hh * 2:4100 + hh * 2 + 2]], axis=1)
    bg = inp["b_gate_if"][l]
    sp = np.concatenate([bg[hh * 2:hh * 2 + 2], bg[4 + hh * 2:4 + hh * 2 + 2], inp["g_mout"][l, hh * 2:hh * 2 + 2].reshape(-1),
                         inp["g_dq"][l], inp["g_dk"][l], inp["lam_q1"][l], inp["lam_k1"][l], inp["lam_q2"][l], inp["lam_k2"][l],
                         inp["g_dsub"][l]]).astype(np.float32)
    wm = inp["w_mconv"][l]
    wcv = np.concatenate([wm[:, hh * 512:(hh + 1) * 512], wm[:, 1024 + hh * 512:1024 + (hh + 1) * 512]], axis=1).T
    return {"wA%d" % l: np.ascontiguousarray(wA), "wIF%d" % l: np.ascontiguousarray(wIF), "sp%d" % l: sp,
            "wcv%d" % l: np.ascontiguousarray(wcv)}


class XF:
    def __init__(self, buf, S, chunk=None):
        self.buf, self.S, self.chunk = buf, S, chunk

    def rows(self, t0, n):
        if self.chunk is None:
            return self.buf.t[t0:t0 + n, :]
        half, XR = self.S // 2, self.chunk
        r, loc = t0 // half, t0 % half
        j, i = loc // XR, loc % XR
        assert i + n <= XR
        o = j * 2 * XR + r * XR + i
        return self.buf.t[o:o + n, :]


def gather_rows(P, src, dst, nrows, chunk, pairs):
    for j in range(nrows // chunk):
        P.kb.coll("AllGather", ALU.bypass, pairs, src.t[j * chunk:(j + 1) * chunk, :], dst.t[j * 2 * chunk:(j + 1) * 2 * chunk, :],
                  [src], [dst], dst)


def ob_chunk(S):
    return min(512, (1 << 20) // S)


def col_tiles(n, tw):
    out = []
    c = 0
    while c < n:
        w = min(tw, n - c)
        out.append((c, w))
        c += w
    return out


def stageB(P, l, xh, xf, modb, gmix, gffn, obG, xn):
    S, kb, pb = P.S, P.kb, P.pb
    half = S // 2
    T = half // 2 if half >= 256 else half
    NPASS = half // T
    TT = min(512, T)
    NB = T // 128
    W = T + 2
    wG = P.dram("wG%d" % l, [D, 6144])
    wBr = P.dram("wBr%d" % l, [3072, D])
    wO = P.dram("wO%d" % l, [D, D])
    wU = P.dram("wU%d" % l, [D, 2 * DFF])
    wD = P.dram("wD%d" % l, [DFF, D])
    wfc = P.dram("wfc%d" % l, [DFF, 3])
    sel = P.dram("sel", [128, 2])
    xmid = P.dram("xmid", [W, D])
    with contextlib.ExitStack() as st0:
        selt = P.sb(st0, "selt", [128, 2], F32)
        wfct = P.sb(st0, "wfct", [128, DFF // 128, 3], F32)
        kb.dma("sp", selt[:, :], sel.t[:, :], reads=[sel], writes=[selt])
        kb.dma("sp", wfct[:, :, :], wfc.t.rearrange("(c p) j -> p c j", p=128), reads=[wfc], writes=[wfct])
        for p in range(NPASS):
            t0 = p * T
            with contextlib.ExitStack() as st:
                hT = P.sb(st, "bhT", [128, 16, W], BF16)
                hTb1 = Buf("bhTb")
                with contextlib.ExitStack() as s2:
                    G, SH = make_G_SH(P, s2, modb, gmix, l, 0)
                    if p == 0:
                        blocks = [(xf.rows(half - 2, 2), xf.buf, 2, 0, False)]
                    else:
                        blocks = [(xh.t[t0 - 2:t0, :], xh, 2, 0, False)]
                    for tb in range(NB):
                        blocks.append((xh.t[t0 + tb * 128:t0 + (tb + 1) * 128, :], xh, 128, 2 + tb * 128, False))
                    norm_pass(P, s2, blocks, hT, lambda c: hTb1, G, SH, pb[0:4])
                    kb.barrier()
                with contextlib.ExitStack() as s1:
                    mT = P.sb(s1, "mT", [128, 16, W], BF16)
                    with contextlib.ExitStack() as s2:
                        obS = P.sb(s2, "obS", [128, 24, W], BF16)
                        with contextlib.ExitStack() as s3:
                            ta = [P.sb(s3, "ta%d" % i, [128, 4, W], BF16) for i in range(2)]
                            tb_ = [P.sb(s3, "tb%d" % i, [128, 4, W], BF16) for i in range(2)]
                            tc = [P.sb(s3, "tc%d" % i, [128, 4, W], BF16) for i in range(2)]
                            n = 0
                            for br in range(3):
                                for r in range(2):
                                    i = n % 2; n += 1
                                    SR = ob_chunk(S)
                                    nsub, cps = 512 // SR, SR // 128
                                    for sub in range(nsub):
                                        j = (br * 512) // SR + sub
                                        rows = obG.t[j * 2 * SR + r * SR:j * 2 * SR + (r + 1) * SR, :].rearrange("(c p) s -> p c s", p=128)
                                        cc = slice(sub * cps, (sub + 1) * cps)
                                        pt = sub > 0
                                        kb.dma("sp", ta[i][:, cc, 2:W], rows[:, :, t0:t0 + T], reads=[obG], writes=[ta[i]], part=pt)
                                        kb.dma("sp", tb_[i][:, cc, 2:W], rows[:, :, half + t0:half + t0 + T], reads=[obG], writes=[tb_[i]], part=pt)
                                        if p == 0:
                                            kb.dma("sp", ta[i][:, cc, 0:2], rows[:, :, half - 2:half], reads=[obG], writes=[ta[i]], part=True)
                                            kb.dma("sp", tb_[i][:, cc, 0:2], rows[:, :, half - 2:half], reads=[obG], writes=[tb_[i]], part=True)
                                        else:
                                            kb.dma("sp", ta[i][:, cc, 0:2], rows[:, :, t0 - 2:t0], reads=[obG], writes=[ta[i]], part=True)
                                            kb.dma("sp", tb_[i][:, cc, 0:2], rows[:, :, half + t0 - 2:half + t0], reads=[obG], writes=[tb_[i]], part=True)
                                    P.act(tc[i][:, :, :], ta[i][:, :, :], AF.Copy, [ta[i], selt], [tc[i]], scale=selt[:, 0:1])
                                    P.stt(obS[:, br * 8 + r * 4:br * 8 + r * 4 + 4, :], tb_[i][:, :, :], selt[:, 1:2], tc[i][:, :, :], ALU.mult, ALU.add,
                                          [tb_[i], selt, tc[i]], [obS])
                            kb.barrier()
                        wg = [P.sb(s2, "wg%d" % i, [128, 3, 16, 256], BF16) for i in range(2)]
                        wbr = [P.sb(s2, "wbr%d" % i, [128, 3, 8, 256], BF16) for i in range(2)]
                        Gs = [P.sb(s2, "Gs%d" % i, [128, TT], F32) for i in range(2)]
                        ac = [P.sb(s2, "ac%d" % i, [128, TT], F32) for i in range(2)]
                        tp = [P.sb(s2, "tp%d" % i, [128, TT], F32) for i in range(2)]
                        tiles = col_tiles(W, TT)
                        it = 0
                        for cg in range(8):
                            w_g, w_b = wg[cg % 2], wbr[cg % 2]
                            for br in range(3):
                                src = wG.t.rearrange("(kc p) n -> p kc n", p=128)[:, :, br * 2048 + cg * 256:br * 2048 + (cg + 1) * 256]
                                kb.dma("pool", w_g[:, br, :, :], src, reads=[wG], writes=[w_g], part=br > 0)
                                src = wBr.t.rearrange("(kc p) n -> p kc n", p=128)[:, br * 8:(br + 1) * 8, cg * 256:(cg + 1) * 256]
                                kb.dma("pool", w_b[:, br, :, :], src, reads=[wBr], writes=[w_b], part=br > 0)
                            for cb in range(2):
                                cs = slice(cb * 128, (cb + 1) * 128)
                                mc = cg * 2 + cb
                                for (c0, w) in tiles:
                                    i = it % 2; it += 1
                                    for br in range(3):
                                        pg, py = pb[(2 * br) % 4 + 0 if False else (br % 2) * 2], pb[(br % 2) * 2 + 1]
                                        for kc in range(16):
                                            P.mm(pg[:, 0:w], w_g[:, br, kc, cs], hT[:, kc, c0:c0 + w], kc == 0, kc == 15, [w_g, hTb1], [pg])
                                        for kc in range(8):
                                            P.mm(py[:, 0:w], w_b[:, br, kc, cs], obS[:, br * 8 + kc, c0:c0 + w], kc == 0, kc == 7, [w_b, obS], [py])
                                        P.act(Gs[i][:, 0:w], pg[:, 0:w], AF.Sigmoid, [pg], [Gs[i]])
                                        if br == 0:
                                            P.tt(ac[i][:, 0:w], Gs[i][:, 0:w], py[:, 0:w], ALU.mult, [Gs[i], py], [ac[i]])
                                        else:
                                            P.tt(tp[i][:, 0:w], Gs[i][:, 0:w], py[:, 0:w], ALU.mult, [Gs[i], py], [tp[i]])
                                            if br == 1:
                                                P.tt(ac[i][:, 0:w], ac[i][:, 0:w], tp[i][:, 0:w], ALU.add, [ac[i], tp[i]], [ac[i]])
                                            else:
                                                P.tt(mT[:, mc, c0:c0 + w], ac[i][:, 0:w], tp[i][:, 0:w], ALU.add, [ac[i], tp[i]], [mT])
                        kb.barrier()
                    with contextlib.ExitStack() as s2:
                        wo = P.sb(s2, "wo", [128, 16, D], BF16)
                        GT = P.sb(s2, "GT1", [128, D], F32)
                        xt = [P.sb(s2, "bxt%d" % i, [128, D], F32) for i in range(2)]
                        xm = [P.sb(s2, "bxm%d" % i, [128, D], F32) for i in range(2)]
                        load_bcast(P, GT, modb.t[2, :], [modb])
                        for q in range(4):
                            src = wO.t.rearrange("(kc p) n -> p kc n", p=128)[:, :, q * 512:(q + 1) * 512]
                            kb.dma("pool", wo[:, :, q * 512:(q + 1) * 512], src, reads=[wO], writes=[wo], part=q > 0)
                        blocks = [(0, 2, (xf.buf, xf.rows(half - 2, 2)) if p == 0 else (xh, xh.t[t0 - 2:t0, :]))]
                        for tb in range(NB):
                            blocks.append((2 + tb * 128, 128, (xh, xh.t[t0 + tb * 128:t0 + (tb + 1) * 128, :])))
                        for bi, (c0, n_, (xsrc, xap)) in enumerate(blocks):
                            i = bi % 2
                            kb.dma("sp", xt[i][0:n_, :], xap, reads=[xsrc], writes=[xt[i]])
                            for q in range(4):
                                po = pb[q]
                                for kc in range(16):
                                    P.mm(po[0:n_, :], mT[:, kc, c0:c0 + n_], wo[:, kc, q * 512:(q + 1) * 512], kc == 0, kc == 15, [mT, wo], [po])
                                qs = slice(q * 512, (q + 1) * 512)
                                P.tt(xm[i][0:n_, qs], po[0:n_, :], GT[0:n_, qs], ALU.mult, [po, GT], [xm[i]])
                                P.tt(xm[i][0:n_, qs], xm[i][0:n_, qs], xt[i][0:n_, qs], ALU.add, [xm[i], xt[i]], [xm[i]])
                            kb.dma("sp", xmid.t[c0:c0 + n_, :], xm[i][0:n_, :], reads=[xm[i]], writes=[xmid], key=xm[i])
                        kb.barrier()
                with contextlib.ExitStack() as s2:
                    G, SH = make_G_SH(P, s2, modb, gffn, l, 1)
                    blocks = [(xmid.t[0:2, :], xmid, 2, 0, False)]
                    for tb in range(NB):
                        blocks.append((xmid.t[2 + tb * 128:2 + (tb + 1) * 128, :], xmid, 128, 2 + tb * 128, False))
                    norm_pass(P, s2, blocks, hT, lambda c: hTb1, G, SH, pb[0:4])
                    kb.barrier()
                with contextlib.ExitStack() as s2:
                    acc = P.sb(s2, "facc", [128, NB, D], F32)
                    wu = [P.sb(s2, "wu%d" % i, [128, 2, 16, 256], BF16) for i in range(2)]
                    wd = [P.sb(s2, "wd%d" % i, [128, 2, D], BF16) for i in range(2)]
                    GP = [P.sb(s2, "GP%d" % i, [128, W], F32) for i in range(2)]
                    CV = [P.sb(s2, "CV%d" % i, [128, T], F32) for i in range(2)]
                    aT = [P.sb(s2, "aT%d" % i, [128, 2, T], BF16) for i in range(2)]
                    tiles = col_tiles(W, TT)
                    mtiles = col_tiles(T, TT)
                    NG = DFF // 256
                    for g in range(NG):
                        w_u, w_d, a_T = wu[g % 2], wd[g % 2], aT[g % 2]
                        for hv in range(2):
                            src = wU.t.rearrange("(kc p) n -> p kc n", p=128)[:, :, hv * DFF + g * 256:hv * DFF + (g + 1) * 256]
                            kb.dma("pool", w_u[:, hv, :, :], src, reads=[wU], writes=[w_u], part=hv > 0)
                        src = wD.t[g * 256:(g + 1) * 256, :].rearrange("(kc p) n -> p kc n", p=128)
                        kb.dma("pool", w_d[:, :, :], src, reads=[wD], writes=[w_d])
                        for cb in range(2):
                            cs = slice(cb * 128, (cb + 1) * 128)
                            ch = g * 2 + cb
                            gp, cv = GP[cb], CV[cb]
                            for ti, (c0, w) in enumerate(tiles):
                                pg = pb[ti % 2]
                                for kc in range(16):
                                    P.mm(pg[:, 0:w], w_u[:, 0, kc, cs], hT[:, kc, c0:c0 + w], kc == 0, kc == 15, [w_u, hTb1], [pg])
                                P.copy(gp[:, c0:c0 + w], pg[:, 0:w], [pg], [gp], eng="act")
                            if p == 0:
                                P.ts(gp[:, 0:2], gp[:, 0:2], selt[:, 1:2], None, ALU.mult, None, [gp, selt], [gp])
                            P.ts(cv[:, :], gp[:, 0:T], wfct[:, ch, 0:1], None, ALU.mult, None, [gp, wfct], [cv])
                            for j in range(1, 3):
                                P.stt(cv[:, :], gp[:, j:T + j], wfct[:, ch, j:j + 1], cv[:, :], ALU.mult, ALU.add, [gp, wfct, cv], [cv])
                            P.act(cv[:, :], cv[:, :], AF.Silu, [cv], [cv])
                            for ti, (c0, w) in enumerate(mtiles):
                                pv = pb[2 + ti % 2]
                                for kc in range(16):
                                    P.mm(pv[:, 0:w], w_u[:, 1, kc, cs], hT[:, kc, 2 + c0:2 + c0 + w], kc == 0, kc == 15, [w_u, hTb1], [pv])
                                P.tt(a_T[:, cb, c0:c0 + w], cv[:, c0:c0 + w], pv[:, 0:w], ALU.mult, [cv, pv], [a_T])
                        for tb in range(NB):
                            for hf in range(2):
                                pd0, pd1 = pb[4 + 2 * ((tb * 2 + hf) % 2)], pb[5 + 2 * ((tb * 2 + hf) % 2)]
                                for q, pd in enumerate((pd0, pd1)):
                                    co = hf * 1024 + q * 512
                                    for kc in range(2):
                                        P.mm(pd[:, :], a_T[:, kc, tb * 128:(tb + 1) * 128], w_d[:, kc, co:co + 512], kc == 0, kc == 1, [a_T, w_d], [pd])
                                    if g == 0:
                                        P.copy(acc[:, tb, co:co + 512], pd[:, :], [pd], [acc], eng="dve")
                                    else:
                                        P.tt(acc[:, tb, co:co + 512], acc[:, tb, co:co + 512], pd[:, :], ALU.add, [acc, pd], [acc])
                    GT = P.sb(s2, "GT2", [128, D], F32)
                    xt = [P.sb(s2, "fxt%d" % i, [128, D], F32) for i in range(2)]
                    load_bcast(P, GT, modb.t[5, :], [modb])
                    for tb in range(NB):
                        i = tb % 2
                        kb.dma("sp", xt[i][:, :], xmid.t[2 + tb * 128:2 + (tb + 1) * 128, :], reads=[xmid], writes=[xt[i]])
                        P.tt(acc[:, tb, :], acc[:, tb, :], GT[:, :], ALU.mult, [acc, GT], [acc])
                        P.tt(xt[i][:, :], xt[i][:, :], acc[:, tb, :], ALU.add, [xt[i], acc], [xt[i]])
                        kb.dma("sp", xn.t[t0 + tb * 128:t0 + (tb + 1) * 128, :], xt[i][:, :], reads=[xt[i]], writes=[xn], key=xt[i])
                    kb.barrier()


def prep_B(inp, l, rank):
    w = inp["w_in"][l]
    sel = np.zeros((128, 2), np.float32)
    sel[:, rank] = 1.0
    return {"wG%d" % l: np.ascontiguousarray(w[:, OFF["gm"]:]), "wBr%d" % l: inp["w_branch"][l].reshape(3072, D),
            "wO%d" % l: inp["w_out"][l], "wU%d" % l: inp["w_up"][l], "wD%d" % l: inp["w_down"][l],
            "wfc%d" % l: np.ascontiguousarray(inp["w_ffconv"][l].T), "sel": sel}


def stageP0(P, modbs, groups8, collective=True):
    kb, pb = P.kb, P.pb
    L = DEPTH
    cm = P.dram("cm", [4, D])
    wa = P.dram("wa", [L, D, 1536])
    ba = P.dram("ba", [L, 1536])
    selb = P.dram("selb", [4, 128])
    sel8 = P.dram("sel8", [4, 8])
    arin = P.dram("arin", [4, L * 8 * 1536])
    aro = P.dram("aro", [4, L * 8 * 1536])
    with contextlib.ExitStack() as st:
        c4 = P.sb(st, "c4", [4, D], F32)
        cT = P.sb(st, "cT", [128, 16, 4], F32)
        s8 = P.sb(st, "s8", [4, 8], F32)
        sb_ = P.sb(st, "selbt", [4, 128], F32)
        msl = P.sb(st, "msl", [4, L * 1536], F32)
        bt = P.sb(st, "bt", [4, L * 1536], F32)
        tmp = [P.sb(st, "p0t%d" % i, [4, L * 1536], F32) for i in range(2)]
        kb.dma("sp", c4[:, :], cm.t[:, :], reads=[cm], writes=[c4])
        kb.dma("sp", s8[:, :], sel8.t[:, :], reads=[sel8], writes=[s8])
        kb.dma("sp", sb_[:, :], selb.t[:, :], reads=[selb], writes=[sb_])
        kb.dma("sp", bt[:, :], ba.t.rearrange("l n -> (l n)").partition_broadcast(4), reads=[ba], writes=[bt])
        P.act(c4[:, :], c4[:, :], AF.Silu, [c4], [c4])
        for kc in range(16):
            pk = pb[kc % 2]
            P.mm(pk[:, 0:4], c4[0:4, kc * 128:(kc + 1) * 128], P.identf[0:4, 0:4], True, True, [c4, P.identf], [pk])
            P.copy(cT[:, kc, :], pk[:, 0:4], [pk], [cT])
        with contextlib.ExitStack() as s2:
            wt = P.sb(s2, "wat", [128, 16, 1536], F32)
            for l in range(L):
                kb.dma("sp", wt[:, :, :], wa.t[l].rearrange("(kc p) n -> p kc n", p=128), reads=[wa], writes=[wt])
                for q in range(3):
                    pk = pb[2 + q % 2]
                    for kc in range(16):
                        P.mm(pk[0:4, :], cT[:, kc, :], wt[:, kc, q * 512:(q + 1) * 512], kc == 0, kc == 15, [cT, wt], [pk])
                    o = l * 1536 + q * 512
                    P.tt(msl[:, o:o + 512], pk[0:4, :], bt[:, o:o + 512], ALU.add, [pk, bt], [msl])
            kb.barrier()
        if not collective:
            mo_ = P.dram("mslo", [4, L * 1536], kind="out")
            kb.dma("sp", mo_.t[:, :], msl[:, :], reads=[msl], writes=[mo_], key=msl)
            kb.wait_all("sp", [mo_])
            return
        av = arin.t.rearrange("b (l s n) -> b l s n", l=L, s=8)
        for s in range(8):
            t = tmp[s % 2]
            P.ts(t[:, :], msl[:, :], s8[:, s:s + 1], None, ALU.mult, None, [msl, s8], [t])
            kb.dma("sp", av[:, :, s, :], t[:, :].rearrange("b (l n) -> b l n", l=L), reads=[t], writes=[arin], key=t)
        kb.barrier()
        kb.coll("AllReduce", ALU.add, groups8, arin.t.opt(), aro.t.opt(), [arin], [aro], aro)
        kb.barrier()
        with contextlib.ExitStack() as s2:
            ar = P.sb(s2, "ar", [4, L * 12288], F32)
            row = [P.sb(s2, "row%d" % i, [1, 2048], F32) for i in range(2)]
            kb.dma("sp", ar[:, :], aro.t[:, :], reads=[aro], writes=[ar])
            n = 0
            for l in range(L):
                for m in range(NMOD):
                    r = row[n % 2]; n += 1
                    for q in range(4):
                        pk = pb[q]
                        o = l * 12288 + m * 2048 + q * 512
                        P.mm(pk[:, :], sb_[0:4, :], ar[0:4, o:o + 512], True, True, [sb_, ar], [pk])
                        P.copy(r[0:1, q * 512:(q + 1) * 512], pk[0:1, :], [pk], [r], eng="act" if q % 2 else "dve")
                    kb.dma("sp", modbs[l].t[m:m + 1, :], r[0:1, :], reads=[r], writes=[modbs[l]], key=r)
            kb.barrier()


def prep_P0(inp, core):
    b = core // 2
    selb = np.zeros((4, 128), np.float32); selb[b, :] = 1.0
    sel8 = np.zeros((4, 8), np.float32); sel8[:, core] = 1.0
    return {"cm": inp["c"], "wa": np.ascontiguousarray(inp["w_ada"][:, :, core * 1536:(core + 1) * 1536]),
            "ba": np.ascontiguousarray(inp["b_ada"][:, core * 1536:(core + 1) * 1536]), "selb": selb, "sel8": sel8}


PAIRS = [[0, 1], [2, 3], [4, 5], [6, 7]]


def build_full(S, use_p0=True, pairs=PAIRS, groups8=None, layers=(0, 1), debug=False):
    half = S // 2
    io = {"xf0": "in", "xh0": "in", "y": "out"}
    if not use_p0:
        io.update({"modb0": "in", "modb1": "in"})
    P = Prog(S, io=io)
    kb = P.kb
    with contextlib.ExitStack() as st:
        P.consts(st)
        gmix = P.dram("gmix", [DEPTH, D])
        gffn = P.dram("gffn", [DEPTH, D])
        modbs = [P.dram("modb%d" % l, [NMOD, D]) for l in range(DEPTH)]
        if use_p0:
            stageP0b(P, modbs, pairs)
        xf = XF(P.dram("xf0", [S, D]), S)
        xh = P.dram("xh0", [half, D])
        for li, l in enumerate(layers):
            last = li == len(layers) - 1
            obT = P.dram("obT%d" % l, [1536, S], BF16)
            obG = P.dram("obG%d" % l, [2 * 1536, S], BF16)
            stageA(P, l, xf, modbs[l], gmix, obT)
            kb.barrier()
            gather_rows(P, obT, obG, 1536, ob_chunk(S), pairs)
            kb.barrier()
            if debug and l == 0:
                d1 = P.dram("dbg_obT", [1536, S], BF16, kind="out")
                d2 = P.dram("dbg_obG", [2 * 1536, S], BF16, kind="out")
                kb.dma("sp", d1.t[:, :], obT.t[:, :], reads=[obT], writes=[d1], key=Buf("kd1"))
                kb.dma("sp", d2.t[:, :], obG.t[:, :], reads=[obG], writes=[d2], key=Buf("kd2"))
                kb.barrier()
            xn = P.dram("y" if last else "xn%d" % l, [half, D])
            stageB(P, l, xh, xf, modbs[l], gmix, gffn, obG, xn)
            kb.barrier()
            if not last:
                xf2 = P.dram("xf%d" % (l + 1), [S, D])
                XR = min(256, half)
                gather_rows(P, xn, xf2, half, XR, pairs)
                kb.barrier()
                if debug:
                    d3 = P.dram("dbg_xn", [half, D], kind="out")
                    d4 = P.dram("dbg_xf", [S, D], kind="out")
                    kb.dma("sp", d3.t[:, :], xn.t[:, :], reads=[xn], writes=[d3], key=Buf("kd3"))
                    kb.dma("sp", d4.t[:, :], xf2.t[:, :], reads=[xf2], writes=[d4], key=Buf("kd4"))
                    kb.barrier()
                xf, xh = XF(xf2, S, XR), xn
        kb.wait_all("sp", [P.drams["y"]])
        kb.emit()
    return P


def prep_core(inp, core, S, use_p0=True, layers=(0, 1), x=None):
    b, r = core // 2, core % 2
    half = S // 2
    x = inp["x"] if x is None else x
    m = {"cst": host_consts(), "gmix": inp["g_mix"], "gffn": inp["g_ffn"],
         "xf0": np.ascontiguousarray(x[b, :S]), "xh0": np.ascontiguousarray(x[b, r * half:(r + 1) * half])}
    for l in layers:
        m.update(prep_A(inp, l, r))
        m.update(prep_B(inp, l, r))
    if use_p0:
        m.update(prep_P0b(inp, core))
    return m


_CACHE = {}


def build_stage(S, which, l=0):
    half = S // 2
    if which == "P0":
        P = Prog(S)
        with contextlib.ExitStack() as st:
            P.consts(st)
            stageP0(P, None, None, collective=False)
            P.kb.emit()
        return P
    if which == "A":
        P = Prog(S, io={"xf0": "in", "modb%d" % l: "in", "obT%d" % l: "out"})
        with contextlib.ExitStack() as st:
            P.consts(st)
            xf = XF(P.dram("xf0", [S, D]), S)
            modb = P.dram("modb%d" % l, [NMOD, D])
            gmix = P.dram("gmix", [DEPTH, D])
            obT = P.dram("obT%d" % l, [1536, S], BF16)
            stageA(P, l, xf, modb, gmix, obT)
            P.kb.wait_all("sp", [obT])
            P.kb.emit()
        return P
    P = Prog(S, io={"xf0": "in", "xh0": "in", "modb%d" % l: "in", "obG%d" % l: "in", "y": "out"})
    with contextlib.ExitStack() as st:
        P.consts(st)
        xf = XF(P.dram("xf0", [S, D]), S)
        xh = P.dram("xh0", [half, D])
        modb = P.dram("modb%d" % l, [NMOD, D])
        gmix = P.dram("gmix", [DEPTH, D])
        gffn = P.dram("gffn", [DEPTH, D])
        obG = P.dram("obG%d" % l, [2 * 1536, S], BF16)
        y = P.dram("y", [half, D])
        stageB(P, l, xh, xf, modb, gmix, gffn, obG, y)
        P.kb.wait_all("sp", [y])
        P.kb.emit()
    return P


def _get(key, fn):
    if key not in _CACHE:
        _CACHE[key] = fn()
    return _CACHE[key]


def kernel_unfused(inp):
    S = inp["x"].shape[1]
    half = S // 2
    cores = list(range(8))
    cst = host_consts()
    P = _get((S, "P0"), lambda: build_stage(S, "P0"))
    maps = []
    for c in cores:
        m = {"cst": cst}
        m.update(prep_P0(inp, c))
        maps.append(m)
    res = run_bass_kernel_spmd(P.nc, maps, core_ids=cores)
    msl = [np.asarray(res.results[c]["mslo"]).reshape(4, DEPTH, 1536) for c in cores]
    mod = np.concatenate(msl, axis=2).reshape(4, DEPTH, NMOD, D)
    x = inp["x"]
    SR = ob_chunk(S)
    for l in range(DEPTH):
        P = _get((S, "A", l), lambda: build_stage(S, "A", l))
        maps = []
        for c in cores:
            b, r = c // 2, c % 2
            m = {"cst": cst, "gmix": inp["g_mix"], "xf0": np.ascontiguousarray(x[b]), "modb%d" % l: np.ascontiguousarray(mod[b, l])}
            m.update(prep_A(inp, l, r))
            maps.append(m)
        res = run_bass_kernel_spmd(P.nc, maps, core_ids=cores)
        obT = [np.asarray(res.results[c]["obT%d" % l]) for c in cores]
        P = _get((S, "B", l), lambda: build_stage(S, "B", l))
        maps = []
        for c in cores:
            b, r = c // 2, c % 2
            obG = np.concatenate([obT[2 * b + rr][j * SR:(j + 1) * SR] for j in range(1536 // SR) for rr in range(2)], axis=0)
            m = {"cst": cst, "gmix": inp["g_mix"], "gffn": inp["g_ffn"], "xf0": np.ascontiguousarray(x[b]),
                 "xh0": np.ascontiguousarray(x[b, r * half:(r + 1) * half]), "modb%d" % l: np.ascontiguousarray(mod[b, l]),
                 "obG%d" % l: obG}
            m.update(prep_B(inp, l, r))
            maps.append(m)
        res = run_bass_kernel_spmd(P.nc, maps, core_ids=cores)
        out = np.empty((4, S, D), np.float32)
        for c in cores:
            b, r = c // 2, c % 2
            out[b, r * half:(r + 1) * half] = np.asarray(res.results[c]["y"])
        x = out
    return x


def kernel_fused(inp):
    S = inp["x"].shape[1]
    half = S // 2
    P = _get((S, "full"), lambda: build_full(S))
    in_maps = [prep_core(inp, c, S) for c in range(8)]
    res = run_bass_kernel_spmd(P.nc, in_maps, core_ids=list(range(8)))
    out = np.empty((4, S, D), np.float32)
    for c in range(8):
        b, r = c // 2, c % 2
        out[b, r * half:(r + 1) * half] = np.asarray(res.results[c]["y"])
    return out


FUSED = True


def kernel(**inputs):
    inp = {k: np.asarray(v) for k, v in inputs.items()}
    return kernel_fused(inp) if FUSED else kernel_unfused(inp)


def stageP0b(P, modbs, pairs):
    kb, pb = P.kb, P.pb
    L = DEPTH
    HC = 6144
    cb = P.dram("cmb", [1, D])
    wa = P.dram("wah", [L, D, HC])
    ba = P.dram("bah", [1, L * HC])
    msd = P.dram("msd", [1, L * HC])
    mga = P.dram("mga", [2, L * HC])
    with contextlib.ExitStack() as st:
        c1 = P.sb(st, "c1", [1, D], F32)
        cT = P.sb(st, "cT1", [128, 16], F32)
        wt = [P.sb(st, "wat%d" % i, [128, 16, 512], F32) for i in range(2)]
        bt = [P.sb(st, "bt1%d" % i, [1, 512], F32) for i in range(2)]
        ot = [P.sb(st, "ot1%d" % i, [1, 512], F32) for i in range(2)]
        kb.dma("sp", c1[:, :], cb.t[:, :], reads=[cb], writes=[c1])
        P.act(c1[:, :], c1[:, :], AF.Silu, [c1], [c1])
        for kc in range(16):
            pk = pb[kc % 2]
            P.mm(pk[:, 0:1], c1[0:1, kc * 128:(kc + 1) * 128], P.identf[0:1, 0:1], True, True, [c1, P.identf], [pk])
            P.copy(cT[:, kc:kc + 1], pk[:, 0:1], [pk], [cT])
        n = 0
        for l in range(L):
            for q in range(HC // 512):
                i = n % 2; n += 1
                w = wt[i]
                src = wa.t[l].rearrange("(kc p) n -> p kc n", p=128)[:, :, q * 512:(q + 1) * 512]
                kb.dma("sp" if i else "act", w[:, :, :], src, reads=[wa], writes=[w])
                o = l * HC + q * 512
                kb.dma("sp", bt[i][:, :], ba.t[:, o:o + 512], reads=[ba], writes=[bt[i]])
                pk = pb[2 + n % 4]
                for kc in range(16):
                    P.mm(pk[0:1, :], cT[:, kc:kc + 1], w[:, kc, :], kc == 0, kc == 15, [cT, w], [pk])
                P.tt(ot[i][:, :], pk[0:1, :], bt[i][:, :], ALU.add, [pk, bt[i]], [ot[i]])
                kb.dma("sp", msd.t[:, o:o + 512], ot[i][:, :], reads=[ot[i]], writes=[msd], key=ot[i])
        kb.barrier()
        kb.coll("AllGather", ALU.bypass, pairs, msd.t[:, :], mga.t[:, :], [msd], [mga], mga)
        kb.barrier()
        stg = [P.sb(st, "mstg%d" % i, [1, HC], F32) for i in range(2)]
        n = 0
        for l in range(L):
            for r in range(2):
                s_ = stg[n % 2]; n += 1
                kb.dma("sp", s_[:, :], mga.t[r:r + 1, l * HC:(l + 1) * HC], reads=[mga], writes=[s_])
                dst = modbs[l].t.rearrange("m d -> (m d)")[r * HC:(r + 1) * HC].rearrange("(o n) -> o n", o=1)
                kb.dma("sp", dst, s_[:, :], reads=[s_], writes=[modbs[l]], key=s_)
        kb.barrier()


def prep_P0b(inp, core):
    b, r = core // 2, core % 2
    return {"cmb": np.ascontiguousarray(inp["c"][b:b + 1]),
            "wah": np.ascontiguousarray(inp["w_ada"][:, :, r * 6144:(r + 1) * 6144]),
            "bah": np.ascontiguousarray(inp["b_ada"][:, r * 6144:(r + 1) * 6144]).reshape(1, -1)}
```

```python
import contextlib
import numpy as np
import concourse.bass as bass
import concourse.mybir as mybir
from concourse.bass_utils import run_bass_kernel_spmd

F32 = mybir.dt.float32
BF16 = mybir.dt.bfloat16
AF = mybir.ActivationFunctionType
ALU = mybir.AluOpType
AX = mybir.AxisListType

D = 2048
DEPTH = 2
MIXW = 1024
DFF = 5632
NMOD = 6
EPS = 1e-6
NIN = 16392
ENGS = ("sp", "act", "dve", "pool", "pe")


class Buf:
    __slots__ = ("name", "t", "w", "r", "dkey")

    def __init__(self, name, t=None):
        self.name = name
        self.t = t
        self.w = []
        self.r = []
        self.dkey = None

    def __getitem__(self, k):
        return self.t[k]


class KB:
    def __init__(self, nc):
        self.nc = nc
        self.ins = {e: [] for e in ENGS}
        self.known = {e: {} for e in ENGS}
        self.dma_cnt = []
        self.need = set()
        self.free_keys = []
        self.pending_keys = []

    def _newkey(self, buf):
        if self.free_keys:
            buf.dkey = self.free_keys.pop()
        else:
            buf.dkey = len(self.dma_cnt)
            self.dma_cnt.append(0)

    def release(self, buf):
        if buf.dkey is not None:
            self.pending_keys.append(buf.dkey)
            buf.dkey = None

    def _deps(self, reads, writes):
        deps = []
        for b in reads:
            deps.extend(b.w)
        for b in writes:
            deps.extend(b.w)
            deps.extend(b.r)
        return deps

    def _waits(self, eng, deps):
        kn = self.known[eng]
        best = {}
        for k, v in deps:
            if eng == "pe" and k == "pe":
                continue
            if kn.get(k, 0) >= v:
                continue
            if best.get(k, 0) < v:
                best[k] = v
        for k, v in best.items():
            kn[k] = v
            if isinstance(k, str):
                self.need.add((k, v))
        return list(best.items())

    def _commit(self, tok, reads, writes):
        for b in reads:
            b.r.append(tok)
            if len(b.r) > 48:
                m = {}
                for k, v in b.r:
                    if m.get(k, 0) < v:
                        m[k] = v
                b.r = list(m.items())
        for b in writes:
            b.w = [tok]
            b.r = []

    def op(self, eng, fn, reads=(), writes=()):
        waits = self._waits(eng, self._deps(reads, writes))
        lst = self.ins[eng]
        idx = len(lst) + 1
        lst.append(("op", fn, waits, idx))
        self._commit((eng, idx), reads, writes)

    def dma(self, eng, out, in_, reads=(), writes=(), key=None, part=False, **kw):
        if key is None:
            key = writes[0] if writes else reads[0]
        if key.dkey is None:
            self._newkey(key)
        waits = self._waits(eng, self._deps(reads, () if part else writes))
        self.dma_cnt[key.dkey] += 16
        tok = (key.dkey, self.dma_cnt[key.dkey])
        lst = self.ins[eng]
        lst.append(("dma", (out, in_, kw), waits, len(lst) + 1, key.dkey))
        self._commit(tok, reads, writes)

    def coll(self, kind, op, groups, in_ap, out_ap, reads, writes, key):
        if key.dkey is None:
            self._newkey(key)
        waits = self._waits("pool", self._deps(reads, writes))
        self.dma_cnt[key.dkey] += 1
        tok = (key.dkey, self.dma_cnt[key.dkey])
        lst = self.ins["pool"]
        lst.append(("coll", (kind, op, groups, in_ap, out_ap), waits, len(lst) + 1, key.dkey))
        self._commit(tok, reads, writes)

    def wait_all(self, eng, bufs):
        deps = []
        for b in bufs:
            deps.extend(b.w)
            deps.extend(b.r)
        waits = self._waits(eng, deps)
        lst = self.ins[eng]
        lst.append(("wait", None, waits, len(lst) + 1))

    def barrier(self):
        deps = []
        for e in ENGS:
            n = 0
            for rec in self.ins[e]:
                if rec[0] == "op":
                    n = rec[3]
            if n:
                deps.append((e, n))
        for k, c in enumerate(self.dma_cnt):
            if c:
                deps.append((k, c))
        for e in ENGS:
            waits = self._waits(e, deps)
            lst = self.ins[e]
            lst.append(("wait", None, waits, len(lst) + 1))
        self.free_keys.extend(self.pending_keys)
        self.pending_keys = []

    def emit(self):
        nc = self.nc
        val = {}
        for e in ENGS:
            v = 0
            for rec in self.ins[e]:
                if (e, rec[3]) in self.need:
                    assert rec[0] == "op"
                    v += 1
                    val[(e, rec[3])] = v
        with contextlib.ExitStack() as st:
            esem = {e: st.enter_context(nc.semaphore("s_" + e)) for e in ENGS}
            dsem = [st.enter_context(nc.semaphore("d%d" % i)) for i in range(len(self.dma_cnt))]
            block = st.enter_context(nc.Block())

            def run(e, eng):
                for rec in self.ins[e]:
                    kind, fn, waits, idx = rec[0], rec[1], rec[2], rec[3]
                    for k, v in waits:
                        if isinstance(k, str):
                            eng.wait_ge(esem[k], val[(k, v)])
                        else:
                            eng.wait_ge(dsem[k], v)
                    if kind == "op":
                        i = fn(eng)
                        if (e, idx) in val:
                            i.then_inc(esem[e], 1)
                    elif kind == "dma":
                        out, in_, kw = fn
                        eng.dma_start(out=out, in_=in_, **kw).then_inc(dsem[rec[4]], 16)
                    elif kind == "coll":
                        ckind, cop, groups, in_ap, out_ap = fn
                        eng.collective_compute(ckind, cop, replica_groups=groups, ins=[in_ap],
                                               outs=[out_ap]).then_inc(dsem[rec[4]], 1)

            @block.sync
            def _(eng):
                run("sp", eng)

            @block.scalar
            def _(eng):
                run("act", eng)

            @block.vector
            def _(eng):
                run("dve", eng)

            @block.gpsimd
            def _(eng):
                run("pool", eng)

            @block.tensor
            def _(eng):
                run("pe", eng)


class Prog:
    def __init__(self, S, io=None, layers=(0, 1)):
        self.S = S
        self.NC = S // 128
        self.io = io or {}
        self.nc = bass.Bass("TRN2", target_bir_lowering=False)
        self.kb = KB(self.nc)
        self.drams = {}
        self.layers = layers
        self.uid = 0

    def dram(self, name, shape, dtype=F32, kind=None):
        if name in self.drams:
            return self.drams[name]
        if name[:2] in ("wA", "wI", "sp", "wc", "gm", "gf", "wG", "wB", "wO", "wU", "wD", "wf", "se", "cm", "wa", "ba"):
            kind = "in"
        kind = {"in": "ExternalInput", "out": "ExternalOutput"}.get(self.io.get(name, kind), "Internal")
        t = self.nc.dram_tensor(name, list(shape), dtype, kind=kind)
        b = Buf(name, t.ap())
        b.t = t.ap()
        self.drams[name] = b
        return b

    def sb(self, st, name, shape, dtype):
        self.uid += 1
        t = st.enter_context(self.nc.sbuf_tensor("%s_%d" % (name, self.uid), list(shape), dtype))
        b = Buf(name, t)
        st.callback(self.kb.release, b)
        return b

    def ps(self, st, name):
        self.uid += 1
        t = st.enter_context(self.nc.psum_tensor("%s_%d" % (name, self.uid), [128, 512], F32))
        return Buf(name, t)

    def mm(self, out, lhsT, rhs, start, stop, reads, writes):
        self.kb.op("pe", lambda e: e.matmul(out, lhsT=lhsT, rhs=rhs, start=start, stop=stop), reads, writes)

    def act(self, out, in_, func, reads, writes, eng="act", **kw):
        self.kb.op(eng, lambda e: e.activation(out=out, in_=in_, func=func, **kw), reads, writes)

    def tt(self, out, in0, in1, op, reads, writes, eng="dve"):
        self.kb.op(eng, lambda e: e.tensor_tensor(out=out, in0=in0, in1=in1, op=op), reads, writes)

    def ts(self, out, in0, s1, s2, op0, op1, reads, writes, eng="dve", **kw):
        if op1 is None:
            self.kb.op(eng, lambda e: e.tensor_scalar(out=out, in0=in0, scalar1=s1, scalar2=None, op0=op0, **kw), reads, writes)
        else:
            self.kb.op(eng, lambda e: e.tensor_scalar(out=out, in0=in0, scalar1=s1, scalar2=s2, op0=op0, op1=op1, **kw), reads, writes)

    def stt(self, out, in0, scalar, in1, op0, op1, reads, writes, eng="dve"):
        self.kb.op(eng, lambda e: e.scalar_tensor_tensor(out=out, in0=in0, scalar=scalar, in1=in1, op0=op0, op1=op1), reads, writes)

    def copy(self, out, in_, reads, writes, eng="dve"):
        if eng == "act":
            self.kb.op("act", lambda e: e.activation(out=out, in_=in_, func=AF.Copy), reads, writes)
        else:
            self.kb.op(eng, lambda e: e.tensor_copy(out=out, in_=in_), reads, writes)

    def memset(self, buf, ap, v, eng="dve"):
        self.kb.op(eng, lambda e: e.memset(ap, v), (), [buf])

    def rstd(self, out, ss, n, reads, writes):
        self.act(out, ss, AF.Sqrt, reads, writes, scale=1.0 / n, bias=EPS)
        self.kb.op("dve", lambda e: e.reciprocal(out=out, in_=out), writes, writes)

    def consts(self, st):
        self.io.setdefault("cst", "in")
        c = self.dram("cst", [128, 5 * 128], F32)
        self.ident = self.sb(st, "ident", [128, 128], BF16)
        self.antid = self.sb(st, "antid", [128, 128], BF16)
        self.m_le = self.sb(st, "m_le", [128, 128], F32)
        self.m_gt = self.sb(st, "m_gt", [128, 128], F32)
        self.tri32 = self.sb(st, "tri32", [128, 128], F32)
        self.ones32 = self.sb(st, "ones32", [128, 128], F32)
        self.identf = self.sb(st, "identf", [128, 128], F32)
        k = self.kb
        k.dma("pool", self.ident[:, :], c[:, 0:128], reads=[c], writes=[self.ident])
        k.dma("pool", self.antid[:, :], c[:, 128:256], reads=[c], writes=[self.antid])
        k.dma("sp", self.m_le[:, :], c[:, 256:384], reads=[c], writes=[self.m_le])
        k.dma("sp", self.m_gt[:, :], c[:, 384:512], reads=[c], writes=[self.m_gt])
        k.dma("sp", self.tri32[:, :], c[:, 256:384], reads=[c], writes=[self.tri32])
        k.dma("sp", self.identf[:, :], c[:, 0:128], reads=[c], writes=[self.identf])
        self.blk32 = self.sb(st, "blk32", [128, 128], F32)
        k.dma("sp", self.blk32[:, :], c[:, 512:640], reads=[c], writes=[self.blk32])
        self.memset(self.ones32, self.ones32[:, :], 1.0)
        self.pb = [self.ps(st, "pb%d" % i) for i in range(8)]
        k.barrier()


def host_consts():
    c = np.zeros((128, 5 * 128), np.float32)
    p = np.arange(128)[:, None]
    j = np.arange(128)[None, :]
    c[:, 0:128] = (p == j)
    c[:, 128:256] = (p + j == 127)
    c[:, 256:384] = (p <= j)
    c[:, 384:512] = (j > p)
    c[:, 512:640] = ((p // 64) == (j // 64))
    return c


def load_bcast(P, dst, row_ap, reads, eng="sp"):
    P.kb.dma(eng, dst[:, :], row_ap.partition_broadcast(128), reads=reads, writes=[dst])


def make_G_SH(P, st, modb, gvec, l, which):
    G = P.sb(st, "G", [128, D], F32)
    SH = P.sb(st, "SH", [128, D], F32)
    with contextlib.ExitStack() as s2:
        tg = P.sb(s2, "tg", [128, D], F32)
        load_bcast(P, SH, modb.t[3 * which + 0, :], [modb])
        load_bcast(P, G, modb.t[3 * which + 1, :], [modb])
        load_bcast(P, tg, gvec.t[l, :], [gvec])
        P.stt(G[:, :], G[:, :], 1.0, tg[:, :], ALU.add, ALU.mult, [G, tg], [G])
        P.kb.barrier()
    return G, SH


def norm_pass(P, st, blocks, hT, hTb_of, G, SH, pbanks):
    with contextlib.ExitStack() as s2:
        xt = [P.sb(s2, "xt%d" % i, [128, D], F32) for i in range(2)]
        tmp = [P.sb(s2, "ntmp%d" % i, [128, D], F32) for i in range(2)]
        yt = [P.sb(s2, "yt%d" % i, [128, D], BF16) for i in range(2)]
        ss = [P.sb(s2, "ss%d" % i, [128, 4], F32) for i in range(2)]
        def stats(bi):
            xap, xbuf, n, dcol, rev = blocks[bi]
            i = bi % 2
            P.kb.dma("sp", xt[i][0:n, :], xap, reads=[xbuf], writes=[xt[i]])
            P.memset(ss[i], ss[i][:, :], 0.0)
            P.act(tmp[i][0:n, :], xt[i][0:n, :], AF.Square, [xt[i]], [tmp[i], ss[i]], accum_out=ss[i][0:n, 0:1])
            P.act(ss[i][0:n, 2:3], ss[i][0:n, 0:1], AF.Sqrt, [ss[i]], [ss[i]], scale=1.0 / D, bias=EPS)

        def apply(bi):
            xap, xbuf, n, dcol, rev = blocks[bi]
            i = bi % 2
            P.kb.op("dve", lambda e, i=i, n=n: e.reciprocal(out=ss[i][0:n, 2:3], in_=ss[i][0:n, 2:3]), [ss[i]], [ss[i]])
            P.stt(tmp[i][0:n, :], xt[i][0:n, :], ss[i][0:n, 2:3], G[0:n, :], ALU.mult, ALU.mult, [xt[i], ss[i], G], [tmp[i]])
            P.tt(yt[i][0:n, :], tmp[i][0:n, :], SH[0:n, :], ALU.add, [tmp[i], SH], [yt[i]])
            idm = P.antid if rev else P.ident
            for g in range(4):
                pb = pbanks[(bi * 4 + g) % len(pbanks)]
                for j in range(4):
                    c = g * 4 + j
                    P.mm(pb[:, j * 128:j * 128 + n], yt[i][0:n, c * 128:(c + 1) * 128], idm[0:n, 0:n] if not rev else idm[0:n, 128 - n:128],
                         True, True, [yt[i], idm], [pb])
                src = pb[:, :].rearrange("p (a b) -> p a b", b=128)[:, :, 0:n]
                dst = hT[:, g * 4:(g + 1) * 4, dcol:dcol + n]
                P.copy(dst, src, [pb], [hTb_of(dcol)], eng="act")

        stats(0)
        for bi in range(len(blocks)):
            if bi + 1 < len(blocks):
                stats(bi + 1)
            apply(bi)


def load_w(P, wbuf, wdram, c0, ncols, kch, first_eng="pool"):
    src = wdram.t.rearrange("(kc p) n -> p kc n", p=128)[:, 0:kch, c0:c0 + ncols]
    P.kb.dma("pool", wbuf[:, 0:kch, 0:ncols], src, reads=[wdram], writes=[wbuf])


SP_BIF, SP_GMOUT, SP_GDQ, SP_GDK, SP_LQ1, SP_LK1, SP_LQ2, SP_LK2, SP_GDSUB, SP_N = 0, 4, 516, 580, 644, 708, 772, 836, 900, 1028


def stageA_proj(P, l, xf, modb, gmix):
    S, NC, kb = P.S, P.NC, P.kb
    TT = min(512, S)
    NT = S // TT
    wA = P.dram("wA%d" % l, [D, 5120])
    wIF = P.dram("wIF%d" % l, [D, 4])
    spk = P.dram("sp%d" % l, [SP_N])
    wcv = P.dram("wcv%d" % l, [1024, 4])
    qkT = P.dram("qkT", [1024, S], BF16)
    mv = P.dram("mv", [S, 512], BF16)
    mos = P.dram("mos", [S, 512], BF16)
    ifg = P.dram("ifg", [S, 4], F32)
    dqk = P.dram("dqkT", [1024, S], BF16)
    dv = P.dram("dv", [S, 512], BF16)
    sqk = P.dram("sqkT", [1024, S], BF16)
    sv = P.dram("sv", [S, 512], BF16)
    pb = P.pb
    with contextlib.ExitStack() as st:
        hT = P.sb(st, "hT", [128, 16, S], BF16)
        hTb = [Buf("hTb%d" % i) for i in range(NT)]
        hTb_of = lambda col: hTb[col // TT]
        wif = P.sb(st, "wif", [128, 16, 4], BF16)
        wcvt = P.sb(st, "wcvt", [128, 8, 4], F32)
        gq = P.sb(st, "gq", [128, 2], F32)
        kb.dma("sp", wcvt[:, :, :], wcv.t.rearrange("(c p) j -> p c j", p=128), reads=[wcv], writes=[wcvt])
        for half in range(2):
            kb.dma("sp", gq[half * 64:(half + 1) * 64, 0:1], spk.t[SP_GDQ:SP_GDQ + 64].rearrange("(p o) -> p o", o=1), reads=[spk], writes=[gq], part=half > 0)
            kb.dma("sp", gq[half * 64:(half + 1) * 64, 1:2], spk.t[SP_GDK:SP_GDK + 64].rearrange("(p o) -> p o", o=1), reads=[spk], writes=[gq], part=True)
        P.ts(gq[:, 0:1], gq[:, 0:1], 0.125, None, ALU.mult, None, [gq], [gq])
        load_w(P, wif, wIF, 0, 4, 16)
        for rev in (False, True):
            with contextlib.ExitStack() as s2:
                G, SH = make_G_SH(P, s2, modb, gmix, l, 0)
                blocks = []
                for tb in range(NC):
                    dcol = (NC - 1 - tb) * 128 if rev else tb * 128
                    blocks.append((xf.rows(tb * 128, 128), xf.buf, 128, dcol, rev))
                norm_pass(P, s2, blocks, hT, hTb_of, G, SH, pb[0:4])
                kb.barrier()
            groups = ([("conv", 0, qkT, 0), ("conv", 1, qkT, 512), ("tm", 2, mv, None), ("sig", 3, mos, None),
                       ("qkn", 4, dqk, 0), ("qkn", 5, dqk, 512), ("tm", 6, dv, None), ("if", None, None, None)]
                      if not rev else
                      [("scl", 7, sqk, 0), ("scl", 8, sqk, 512), ("tm", 9, sv, None)])
            with contextlib.ExitStack() as s2:
                wb = [P.sb(s2, "wb%d" % i, [128, 16, 512], BF16) for i in range(2)]
                ZB = P.sb(s2, "ZB", [128, S + 3], F32)
                CH = max(S // 2, 128)
                CA = P.sb(s2, "CA", [128, CH], F32)
                QO = [P.sb(s2, "QO%d" % i, [128, S], BF16) for i in range(1)]
                SQ = [P.sb(s2, "SQ%d" % i, [128, TT], F32) for i in range(1)]
                RS = [P.sb(s2, "RS%d" % i, [128, TT], F32) for i in range(1)]
                stg = [P.sb(s2, "stg%d" % i, [128, 512], BF16) for i in range(2)]
                ifs = [P.sb(s2, "ifs%d" % i, [128, 4], F32) for i in range(2)]
                P.memset(ZB, ZB[:, 0:3], 0.0)
                pi = 0
                qi = 0
                for gi, (kind, g, dst, roff) in enumerate(groups):
                    if kind == "if":
                        for tb in range(NC):
                            pbk = pb[pi % 4]; pi += 1
                            for kc in range(16):
                                P.mm(pbk[:, 0:4], hT[:, kc, tb * 128:(tb + 1) * 128], wif[:, kc, :], kc == 0, kc == 15,
                                     [hTb_of(tb * 128), wif], [pbk])
                            s = ifs[tb % 2]
                            P.copy(s[:, :], pbk[:, 0:4], [pbk], [s])
                            kb.dma("sp", ifg.t[tb * 128:(tb + 1) * 128, :], s[:, :], reads=[s], writes=[ifg], key=s)
                        continue
                    w = wb[gi % 2]
                    load_w(P, w, wA, g * 512, 512, 16)
                    if kind in ("tm", "sig"):
                        for tb in range(NC):
                            pbk = pb[pi % 4]; pi += 1
                            for kc in range(16):
                                P.mm(pbk[:, :], hT[:, kc, tb * 128:(tb + 1) * 128], w[:, kc, :], kc == 0, kc == 15,
                                     [hTb_of(tb * 128), w], [pbk])
                            s = stg[tb % 2]
                            if kind == "sig":
                                P.act(s[:, :], pbk[:, :], AF.Sigmoid, [pbk], [s])
                            else:
                                P.copy(s[:, :], pbk[:, :], [pbk], [s], eng="act" if tb % 2 else "dve")
                            drow = (NC - 1 - tb) if False else tb
                            kb.dma("sp", dst.t[drow * 128:(drow + 1) * 128, :], s[:, :], reads=[s], writes=[dst], key=s)
                        continue
                    for cb in range(4):
                        qo = QO[0]; qi += 1
                        for tt in range(NT):
                            pbk = pb[pi % 4]; pi += 1
                            for kc in range(16):
                                P.mm(pbk[:, 0:TT], w[:, kc, cb * 128:(cb + 1) * 128], hT[:, kc, tt * TT:(tt + 1) * TT], kc == 0, kc == 15,
                                     [w, hTb[tt]], [pbk])
                            if kind == "conv":
                                P.copy(ZB[:, 3 + tt * TT:3 + (tt + 1) * TT], pbk[:, 0:TT], [pbk], [ZB], eng="act")
                            elif kind == "scl":
                                P.act(qo[:, tt * TT:(tt + 1) * TT], pbk[:, 0:TT], AF.Copy, [pbk], [qo],
                                      scale=(128.0 ** -0.5) if g == 7 else 1.0)
                            else:
                                sq = SQ[0]; rs = RS[0]
                                P.act(sq[:, 0:TT], pbk[:, 0:TT], AF.Square, [pbk], [sq])
                                p2 = pb[4 + (pi % 2)]
                                P.mm(p2[:, 0:TT], P.blk32[:, :], sq[:, 0:TT], True, True, [P.blk32, sq], [p2])
                                P.rstd(rs[:, 0:TT], p2[:, 0:TT], 64, [p2], [rs])
                                gcol = gq[:, 0:1] if g == 4 else gq[:, 1:2]
                                P.stt(qo[:, tt * TT:(tt + 1) * TT], pbk[:, 0:TT], gcol, rs[:, 0:TT], ALU.mult, ALU.mult,
                                      [pbk, gq, rs], [qo])
                        if kind == "conv":
                            ch = g * 4 + cb
                            for hv in range(S // CH):
                                o = hv * CH
                                P.ts(CA[:, :], ZB[:, o:o + CH], wcvt[:, ch, 0:1], None, ALU.mult, None, [ZB, wcvt], [CA])
                                for j in range(1, 4):
                                    P.stt(CA[:, :], ZB[:, o + j:o + CH + j], wcvt[:, ch, j:j + 1], CA[:, :], ALU.mult, ALU.add, [ZB, wcvt, CA], [CA])
                                P.act(qo[:, o:o + CH], CA[:, :], AF.Silu, [CA], [qo])
                        r0 = roff + cb * 128
                        kb.dma("sp", dst.t[r0:r0 + 128, :], qo[:, :], reads=[qo], writes=[dst], key=qo)
                kb.barrier()


def stageA_mlstm(P, l, obT):
    S, NC, kb = P.S, P.NC, P.kb
    pb = P.pb
    spk = P.dram("sp%d" % l, [SP_N])
    qkT = P.dram("qkT", [1024, S], BF16)
    mv = P.dram("mv", [S, 512], BF16)
    mos = P.dram("mos", [S, 512], BF16)
    ifg = P.dram("ifg", [S, 4], F32)
    with contextlib.ExitStack() as st:
        IFt = P.sb(st, "IFt", [128, NC, 4], F32)
        LF = P.sb(st, "LF", [128, NC, 2], F32)
        KS = P.sb(st, "KS", [128, NC, 2], F32)
        QS = P.sb(st, "QS", [128, NC, 2], F32)
        CDc = P.sb(st, "CDc", [128, NC, 2], F32)
        bifb = P.sb(st, "bifb", [128, 4], F32)
        gmo = P.sb(st, "gmo", [128, 512], F32)
        kb.dma("sp", IFt[:, :, :], ifg.t.rearrange("(c p) f -> p c f", p=128), reads=[ifg], writes=[IFt])
        load_bcast(P, bifb, spk.t[SP_BIF:SP_BIF + 4], [spk])
        load_bcast(P, gmo, spk.t[SP_GMOUT:SP_GMOUT + 512], [spk])
        for j in range(4):
            P.ts(IFt[:, :, j], IFt[:, :, j], bifb[:, j:j + 1], None, ALU.add, None, [IFt, bifb], [IFt])
        P.act(LF[:, :, :], IFt[:, :, 2:4], AF.Exp, [IFt], [LF], scale=-1.0)
        P.act(LF[:, :, :], LF[:, :, :], AF.Ln, [LF], [LF], bias=1.0)
        P.ts(LF[:, :, :], LF[:, :, :], -1.0, None, ALU.mult, None, [LF], [LF])
        lf2 = LF[:, :, :].rearrange("p c h -> p (c h)")
        P.mm(pb[0][:, 0:NC * 2], P.tri32[:, :], lf2, True, True, [P.tri32, LF], [pb[0]])
        P.mm(pb[1][:, 0:NC * 2], P.ones32[:, :], lf2, True, True, [P.ones32, LF], [pb[1]])
        Bv = pb[0][:, 0:NC * 2].rearrange("p (c h) -> p c h", h=2)
        BLv = pb[1][:, 0:NC * 2].rearrange("p (c h) -> p c h", h=2)
        P.tt(KS[:, :, :], IFt[:, :, 0:2], Bv, ALU.subtract, [IFt, pb[0]], [KS])
        P.act(KS[:, :, :], KS[:, :, :], AF.Exp, [KS], [KS])
        P.act(QS[:, :, :], Bv, AF.Exp, [pb[0]], [QS])
        P.ts(QS[:, :, :], QS[:, :, :], 1.0 / 16, None, ALU.mult, None, [QS], [QS])
        P.act(CDc[:, :, :], BLv, AF.Exp, [pb[1]], [CDc])
        kb.barrier()
        for h in range(2):
            with contextlib.ExitStack() as s2:
                qT = P.sb(s2, "qT", [128, 2, S], BF16)
                kT = P.sb(s2, "kT", [128, 2, S], BF16)
                Vp = P.sb(s2, "Vp", [128, NC, 257], BF16)
                VS = P.sb(s2, "VS", [128, NC, 257], BF16)
                MO = P.sb(s2, "MO", [128, NC, 256], BF16)
                ktm = P.sb(s2, "ktm", [128, NC, 256], BF16)
                obm = P.sb(s2, "obm", [128, 2, S], BF16)
                Cs = P.sb(s2, "Cs", [128, 2, 257], F32)
                Cb = P.sb(s2, "Cb", [128, 2, 257], BF16)
                wT = [P.sb(s2, "wT%d" % i, [128, 128], BF16) for i in range(2)]
                hm = [P.sb(s2, "hm%d" % i, [128, 256], F32) for i in range(2)]
                tm2 = [P.sb(s2, "tm2%d" % i, [128, 256], F32) for i in range(2)]
                om = [P.sb(s2, "om%d" % i, [128, 256], BF16) for i in range(2)]
                sc = [P.sb(s2, "sc%d" % i, [128, 8], F32) for i in range(2)]
                kb.dma("sp", qT[:, :, :], qkT.t[h * 256:(h + 1) * 256, :].rearrange("(dc p) s -> p dc s", p=128), reads=[qkT], writes=[qT])
                kb.dma("sp", kT[:, :, :], qkT.t[512 + h * 256:512 + (h + 1) * 256, :].rearrange("(dc p) s -> p dc s", p=128), reads=[qkT], writes=[kT])
                kb.dma("sp", Vp[:, :, 0:256], mv.t[:, h * 256:(h + 1) * 256].rearrange("(c p) d -> p c d", p=128), reads=[mv], writes=[Vp])
                kb.dma("sp", MO[:, :, :], mos.t[:, h * 256:(h + 1) * 256].rearrange("(c p) d -> p c d", p=128), reads=[mos], writes=[MO])
                P.memset(Vp, Vp[:, :, 256:257], 1.0, eng="pool")
                P.memset(Cs, Cs[:, :, :], 0.0)
                for c in range(NC):
                    pk = pb[2 + c % 2]
                    for dc in range(2):
                        P.mm(pk[:, dc * 128:(dc + 1) * 128], kT[:, dc, c * 128:(c + 1) * 128], P.ident[:, :], True, True, [kT, P.ident], [pk])
                    P.copy(ktm[:, c, :], pk[:, 0:256], [pk], [ktm], eng="act" if c % 2 else "dve")
                    P.act(VS[:, c, :], Vp[:, c, :], AF.Copy, [Vp, KS], [VS], scale=KS[:, c, h:h + 1])
                for c in range(NC):
                    cs = slice(c * 128, (c + 1) * 128)
                    i = c % 2
                    pS, pN, pD0, pD1, pT = pb[0], pb[1], pb[4], pb[5], pb[6 + i]
                    for dc in range(2):
                        P.mm(pS[:, 0:128], kT[:, dc, cs], qT[:, dc, cs], dc == 0, dc == 1, [kT, qT], [pS])
                    P.stt(wT[i][:, :], pS[:, 0:128], KS[:, c, h:h + 1], P.m_le[:, :], ALU.mult, ALU.mult, [pS, KS, P.m_le], [wT[i]])
                    P.mm(pN[:, 0:257], wT[i][:, :], Vp[:, c, :], True, c == 0, [wT[i], Vp], [pN])
                    if c > 0:
                        for dc in range(2):
                            P.mm(pN[:, 0:257], qT[:, dc, cs], Cb[:, dc, :], False, dc == 1, [qT, Cb], [pN])
                    s_ = sc[i]
                    P.tt(s_[:, 0:1], pN[:, 256:257], QS[:, c, h:h + 1], ALU.mult, [pN, QS], [s_])
                    P.act(s_[:, 1:2], s_[:, 0:1], AF.Abs, [s_], [s_])
                    P.ts(s_[:, 1:2], s_[:, 1:2], 1.0, None, ALU.max, None, [s_], [s_])
                    kb.op("dve", lambda e, s_=s_: e.reciprocal(out=s_[:, 2:3], in_=s_[:, 1:2]), [s_], [s_])
                    P.tt(s_[:, 3:4], s_[:, 2:3], QS[:, c, h:h + 1], ALU.mult, [s_, QS], [s_])
                    P.ts(hm[i][:, :], pN[:, 0:256], s_[:, 3:4], None, ALU.mult, None, [pN, s_], [hm[i]])
                    for dc, pD in ((0, pD0), (1, pD1)):
                        P.mm(pD[:, 0:257], ktm[:, c, dc * 128:(dc + 1) * 128], VS[:, c, :], True, True, [ktm, VS], [pD])
                        P.tt(Cs[:, dc, :], Cs[:, dc, :], pD[:, 0:257], ALU.add, [Cs, pD], [Cs])
                    P.ts(Cs[:, :, :], Cs[:, :, :], CDc[:, c, h:h + 1], None, ALU.mult, None, [Cs, CDc], [Cs])
                    P.copy(Cb[:, :, :], Cs[:, :, :], [Cs], [Cb], eng="act")
                    P.memset(s_, s_[:, 4:5], 0.0)
                    P.act(tm2[i][:, :], hm[i][:, :], AF.Square, [hm[i]], [tm2[i], s_], accum_out=s_[:, 4:5])
                    P.rstd(s_[:, 6:7], s_[:, 4:5], 256, [s_], [s_])
                    P.stt(tm2[i][:, :], hm[i][:, :], s_[:, 6:7], gmo[:, h * 256:(h + 1) * 256], ALU.mult, ALU.mult, [hm[i], s_, gmo], [tm2[i]])
                    P.tt(om[i][:, :], tm2[i][:, :], MO[:, c, :], ALU.mult, [tm2[i], MO], [om[i]])
                    for dc in range(2):
                        P.mm(pT[:, dc * 128:(dc + 1) * 128], om[i][:, dc * 128:(dc + 1) * 128], P.ident[:, :], True, True, [om[i], P.ident], [pT])
                    P.copy(obm[:, :, cs], pT[:, 0:256].rearrange("p (a b) -> p a b", b=128), [pT], [obm], eng="act")
                kb.dma("sp", obT.t[h * 256:(h + 1) * 256, :].rearrange("(dc p) s -> p dc s", p=128), obm[:, :, :], reads=[obm], writes=[obT], key=obm)
                kb.barrier()


def stageA_diff(P, l, obT):
    import math
    S, NC, kb = P.S, P.NC, P.kb
    pb = P.pb
    TT = min(512, S)
    NP = S // TT
    nqb = TT // 128
    lam_init = 0.8 - 0.6 * math.exp(-0.3 * l)
    spk = P.dram("sp%d" % l, [SP_N])
    dqk = P.dram("dqkT", [1024, S], BF16)
    dv = P.dram("dv", [S, 512], BF16)
    with contextlib.ExitStack() as st:
        lt = [P.sb(st, "lt%d" % i, [128, 64], F32) for i in range(4)]
        lam = P.sb(st, "lam", [128, 8], F32)
        gds = P.sb(st, "gds", [128, 128], F32)
        for i, off in enumerate((SP_LQ1, SP_LK1, SP_LQ2, SP_LK2)):
            load_bcast(P, lt[i], spk.t[off:off + 64], [spk])
        load_bcast(P, gds, spk.t[SP_GDSUB:SP_GDSUB + 128], [spk])
        P.ts(gds[:, :], gds[:, :], 1.0 - lam_init, None, ALU.mult, None, [gds], [gds])
        P.tt(lt[0][:, :], lt[0][:, :], lt[1][:, :], ALU.mult, [lt[0], lt[1]], [lt[0]])
        P.tt(lt[2][:, :], lt[2][:, :], lt[3][:, :], ALU.mult, [lt[2], lt[3]], [lt[2]])
        kb.op("dve", lambda e: e.reduce_sum(out=lam[:, 0:1], in_=lt[0][:, :], axis=AX.X), [lt[0]], [lam])
        kb.op("dve", lambda e: e.reduce_sum(out=lam[:, 1:2], in_=lt[2][:, :], axis=AX.X), [lt[2]], [lam])
        P.act(lam[:, 2:4], lam[:, 0:2], AF.Exp, [lam], [lam])
        P.tt(lam[:, 4:5], lam[:, 3:4], lam[:, 2:3], ALU.subtract, [lam], [lam])
        P.ts(lam[:, 5:6], lam[:, 4:5], -lam_init, None, ALU.add, None, [lam], [lam])
        kb.barrier()
        for h in range(4):
            with contextlib.ExitStack() as s2:
                qT = P.sb(s2, "dqT", [128, S], BF16)
                kT = P.sb(s2, "dkT", [128, S], BF16)
                Vp = P.sb(s2, "dVp", [128, NC, 129], BF16)
                obd = P.sb(s2, "obd", [128, S], BF16)
                Eb = [P.sb(s2, "Eb%d" % i, [128, TT], BF16) for i in range(3)]
                O = [P.sb(s2, "O%d" % i, [128, nqb, 128], F32) for i in range(2)]
                rc = [P.sb(s2, "rc%d" % i, [128, 8], F32) for i in range(2)]
                od = [P.sb(s2, "od%d" % i, [128, 128], F32) for i in range(2)]
                t2 = [P.sb(s2, "t2%d" % i, [128, 128], F32) for i in range(2)]
                ob = [P.sb(s2, "ob%d" % i, [128, 128], BF16) for i in range(2)]
                kb.dma("sp", qT[:, :], dqk.t[h * 128:(h + 1) * 128, :], reads=[dqk], writes=[qT])
                kb.dma("sp", kT[:, :], dqk.t[512 + h * 128:512 + (h + 1) * 128, :], reads=[dqk], writes=[kT])
                kb.dma("sp", Vp[:, :, 0:128], dv.t[:, h * 128:(h + 1) * 128].rearrange("(c p) d -> p c d", p=128), reads=[dv], writes=[Vp])
                P.memset(Vp, Vp[:, :, 128:129], 1.0, eng="pool")
                ei = 0
                for qp in range(NP):
                    for c in range(2):
                        ps_ = slice(c * 64, (c + 1) * 64)
                        nk = (qp + 1) * nqb
                        def front(kbk):
                            q0 = max(0, kbk - qp * nqb)
                            w = TT - q0 * 128
                            qc0 = qp * TT + q0 * 128
                            pS = (pb[0], pb[1], pb[6])[kbk % 3]
                            P.mm(pS[:, 0:w], kT[ps_, kbk * 128:(kbk + 1) * 128], qT[ps_, qc0:qc0 + w], True, True, [kT, qT], [pS])
                            E = Eb[kbk % 3]
                            P.act(E[:, 0:w], pS[:, 0:w], AF.Exp, [pS], [E])
                            if kbk >= qp * nqb:
                                P.tt(E[:, 0:128], E[:, 0:128], P.m_le[:, :], ALU.mult, [E, P.m_le], [E], eng="pool")

                        def back(kbk):
                            q0 = max(0, kbk - qp * nqb)
                            E = Eb[kbk % 3]
                            for qb in range(q0, nqb):
                                pa = pb[2 + qb]
                                P.mm(pa[:, 0:129], E[:, (qb - q0) * 128:(qb - q0 + 1) * 128], Vp[:, kbk, :], kbk == 0, kbk == qp * nqb + qb, [E, Vp], [pa])

                        front(0)
                        if nk > 1:
                            front(1)
                        for kbk in range(nk):
                            if kbk + 2 < nk:
                                front(kbk + 2)
                            back(kbk)
                        for qb in range(nqb):
                            pa = pb[2 + qb]
                            r = rc[qb % 2]
                            kb.op("dve", lambda e, r=r, pa=pa: e.reciprocal(out=r[:, 0:1], in_=pa[:, 128:129]), [pa], [r])
                            P.ts(O[c][:, qb, :], pa[:, 0:128], r[:, 0:1], None, ALU.mult, None, [pa, r], [O[c]])
                    for qb in range(nqb):
                        i = qb % 2
                        r = rc[i]
                        P.stt(od[i][:, :], O[1][:, qb, :], lam[:, 5:6], O[0][:, qb, :], ALU.mult, ALU.add, [O[0], O[1], lam], [od[i]])
                        P.memset(r, r[:, 1:2], 0.0)
                        P.act(t2[i][:, :], od[i][:, :], AF.Square, [od[i]], [t2[i], r], accum_out=r[:, 1:2])
                        P.rstd(r[:, 3:4], r[:, 1:2], 128, [r], [r])
                        P.stt(ob[i][:, :], od[i][:, :], r[:, 3:4], gds[:, :], ALU.mult, ALU.mult, [od[i], r, gds], [ob[i]])
                        pT = pb[7]
                        P.mm(pT[:, 0:128], ob[i][:, :], P.ident[:, :], True, True, [ob[i], P.ident], [pT])
                        c0 = qp * TT + qb * 128
                        P.copy(obd[:, c0:c0 + 128], pT[:, 0:128], [pT], [obd], eng="act")
                kb.dma("sp", obT.t[512 + h * 128:512 + (h + 1) * 128, :], obd[:, :], reads=[obd], writes=[obT], key=obd)
                kb.barrier()


def stageA_sb(P, l, obT):
    S, NC, kb = P.S, P.NC, P.kb
    pb = P.pb
    sqk = P.dram("sqkT", [1024, S], BF16)
    sv = P.dram("sv", [S, 512], BF16)
    with contextlib.ExitStack() as st:
        ones = P.sb(st, "sones", [128, 512], F32)
        P.memset(ones, ones[:, :], 1.0)
        for h in range(4):
            with contextlib.ExitStack() as s2:
                qT = P.sb(s2, "sqT", [128, S], BF16)
                kT = P.sb(s2, "skT", [128, S], BF16)
                V = P.sb(s2, "sV", [128, NC, 128], BF16)
                obs = P.sb(s2, "obs", [128, S], BF16)
                Lb = [P.sb(s2, "Lb%d" % i, [128, 512], F32) for i in range(3)]
                CSb = [P.sb(s2, "CSb%d" % i, [128, 512], F32) for i in range(2)]
                Wb = [P.sb(s2, "Wb%d" % i, [128, 512], F32) for i in range(2)]
                Ab = [P.sb(s2, "Ab%d" % i, [128, 512], BF16) for i in range(2)]
                ATb = [P.sb(s2, "ATb%d" % i, [128, 512], BF16) for i in range(2)]
                osb = [P.sb(s2, "osb%d" % i, [128, 128], BF16) for i in range(2)]
                kb.dma("sp", qT[:, :], sqk.t[h * 128:(h + 1) * 128, :], reads=[sqk], writes=[qT])
                kb.dma("sp", kT[:, :], sqk.t[512 + h * 128:512 + (h + 1) * 128, :], reads=[sqk], writes=[kT])
                kb.dma("sp", V[:, :, :], sv.t[:, h * 128:(h + 1) * 128].rearrange("(c p) d -> p c d", p=128), reads=[sv], writes=[V])
                pieces = []
                for rb in range(NC):
                    k0 = rb * 128
                    first = True
                    while k0 < S:
                        w = min(512, S - k0)
                        pieces.append((rb, k0, w, first, k0 + w >= S))
                        first = False
                        k0 += w
                state = {"carry": None}

                def front(n):
                    rb, k0, w, first, last = pieces[n]
                    L_ = Lb[n % 3]
                    pz = pb[n % 3]
                    P.mm(pz[:, 0:w], qT[:, rb * 128:(rb + 1) * 128], kT[:, k0:k0 + w], True, True, [qT, kT], [pz])
                    P.act(L_[:, 0:w], pz[:, 0:w], AF.Exp, [pz], [L_])
                    P.act(L_[:, 0:w], L_[:, 0:w], AF.Ln, [L_], [L_], bias=1.0)
                    if first:
                        P.tt(L_[:, 0:128], L_[:, 0:128], P.m_gt[:, :], ALU.mult, [L_, P.m_gt], [L_], eng="pool")

                def back(n):
                    rb, k0, w, first, last = pieces[n]
                    i = n % 2
                    pz = pb[n % 3]
                    L_ = Lb[n % 3]
                    po = pb[5 + rb % 2]
                    carry = None if first else state["carry"]
                    rd = [ones, L_] + ([] if carry is None else [carry[1]])
                    init = 0.0 if carry is None else carry[0]
                    kb.op("dve", lambda e, i=i, w=w, init=init, L_=L_: e.tensor_tensor_scan(
                        out=CSb[i][:, 0:w], data0=ones[:, 0:w], data1=L_[:, 0:w], initial=init, op0=ALU.mult, op1=ALU.add), rd, [CSb[i]])
                    state["carry"] = (CSb[i][:, w - 1:w], CSb[i])
                    P.tt(Wb[i][:, 0:w], pz[:, 0:w], CSb[i][:, 0:w], ALU.subtract, [pz, CSb[i]], [Wb[i]])
                    P.act(Ab[i][:, 0:w], Wb[i][:, 0:w], AF.Exp, [Wb[i]], [Ab[i]])
                    if first:
                        P.tt(Ab[i][:, 0:128], Ab[i][:, 0:128], P.m_gt[:, :], ALU.mult, [Ab[i], P.m_gt], [Ab[i]], eng="pool")

                def back2(n):
                    rb, k0, w, first, last = pieces[n]
                    i = n % 2
                    po = pb[5 + rb % 2]
                    pT = pb[3 + i]
                    nb = w // 128
                    for j in range(nb):
                        P.mm(pT[:, j * 128:(j + 1) * 128], Ab[i][:, j * 128:(j + 1) * 128], P.ident[:, :], True, True, [Ab[i], P.ident], [pT])
                    P.copy(ATb[i][:, 0:w], pT[:, 0:w], [pT], [ATb[i]], eng="dve")
                    for j in range(nb):
                        kblk = k0 // 128 + j
                        P.mm(po[:, 0:128], ATb[i][:, j * 128:(j + 1) * 128], V[:, kblk, :], first and j == 0, last and j == nb - 1, [ATb[i], V], [po])
                    if last:
                        o = osb[rb % 2]
                        P.copy(o[:, :], po[:, 0:128], [po], [o], eng="act")
                        pT2 = pb[7]
                        P.mm(pT2[:, 0:128], o[:, :], P.antid[:, :], True, True, [o, P.antid], [pT2])
                        c0 = (NC - 1 - rb) * 128
                        P.copy(obs[:, c0:c0 + 128], pT2[:, 0:128], [pT2], [obs], eng="act")

                NPc = len(pieces)
                front(0)
                if NPc > 1:
                    front(1)
                back(0)
                for n in range(NPc):
                    if n + 2 < NPc:
                        front(n + 2)
                    if n + 1 < NPc:
                        back(n + 1)
                    back2(n)
                kb.dma("sp", obT.t[1024 + h * 128:1024 + (h + 1) * 128, :], obs[:, :], reads=[obs], writes=[obT], key=obs)
                kb.barrier()


def stageA(P, l, xf, modb, gmix, obT):
    stageA_proj(P, l, xf, modb, gmix)
    stageA_mlstm(P, l, obT)
    stageA_diff(P, l, obT)
    stageA_sb(P, l, obT)


OFF = dict(mq=0, mk=1024, mv=2048, mo=3072, mi=4096, mf=4100, dq=4104, dk=5128, dv=6152, sq=7176, sk=8200, sv=9224,
           gm=10248, gd=12296, gs=14344)


def prep_A(inp, l, hh):
    w = inp["w_in"][l]
    sl = lambda o: w[:, o + hh * 512:o + (hh + 1) * 512]
    wA = np.concatenate([sl(OFF[k]) for k in ("mq", "mk", "mv", "mo", "dq", "dk", "dv", "sq", "sk", "sv")], axis=1)
    wIF = np.concatenate([w[:, 4096 + hh * 2:4096 + hh * 2 + 2], w[:, 4100 + hh * 2:4100 + hh * 2 + 2]], axis=1)
    bg = inp["b_gate_if"][l]
    sp = np.concatenate([bg[hh * 2:hh * 2 + 2], bg[4 + hh * 2:4 + hh * 2 + 2], inp["g_mout"][l, hh * 2:hh * 2 + 2].reshape(-1),
                         inp["g_dq"][l], inp["g_dk"][l], inp["lam_q1"][l], inp["lam_k1"][l], inp["lam_q2"][l], inp["lam_k2"][l],
                         inp["g_dsub"][l]]).astype(np.float32)
    wm = inp["w_mconv"][l]
    wcv = np.concatenate([wm[:, hh * 512:(hh + 1) * 512], wm[:, 1024 + hh * 512:1024 + (hh + 1) * 512]], axis=1).T
    return {"wA%d" % l: np.ascontiguousarray(wA), "wIF%d" % l: np.ascontiguousarray(wIF), "sp%d" % l: sp,
            "wcv%d" % l: np.ascontiguousarray(wcv)}


class XF:
    def __init__(self, buf, S, chunk=None):
        self.buf, self.S, self.chunk = buf, S, chunk

    def rows(self, t0, n):
        if self.chunk is None:
            return self.buf.t[t0:t0 + n, :]
        half, XR = self.S // 2, self.chunk
        r, loc = t0 // half, t0 % half
        j, i = loc // XR, loc % XR
        assert i + n <= XR
        o = j * 2 * XR + r * XR + i
        return self.buf.t[o:o + n, :]


def gather_rows(P, src, dst, nrows, chunk, pairs):
    for j in range(nrows // chunk):
        P.kb.coll("AllGather", ALU.bypass, pairs, src.t[j * chunk:(j + 1) * chunk, :], dst.t[j * 2 * chunk:(j + 1) * 2 * chunk, :],
                  [src], [dst], dst)


def ob_chunk(S):
    return min(512, (1 << 20) // S)


def col_tiles(n, tw):
    out = []
    c = 0
    while c < n:
        w = min(tw, n - c)
        out.append((c, w))
        c += w
    return out


def stageB(P, l, xh, xf, modb, gmix, gffn, obG, xn):
    S, kb, pb = P.S, P.kb, P.pb
    half = S // 2
    T = half // 2 if half >= 256 else half
    NPASS = half // T
    TT = min(512, T)
    NB = T // 128
    W = T + 2
    wG = P.dram("wG%d" % l, [D, 6144])
    wBr = P.dram("wBr%d" % l, [3072, D])
    wO = P.dram("wO%d" % l, [D, D])
    wU = P.dram("wU%d" % l, [D, 2 * DFF])
    wD = P.dram("wD%d" % l, [DFF, D])
    wfc = P.dram("wfc%d" % l, [DFF, 3])
    sel = P.dram("sel", [128, 2])
    xmid = P.dram("xmid", [W, D])
    with contextlib.ExitStack() as st0:
        selt = P.sb(st0, "selt", [128, 2], F32)
        wfct = P.sb(st0, "wfct", [128, DFF // 128, 3], F32)
        kb.dma("sp", selt[:, :], sel.t[:, :], reads=[sel], writes=[selt])
        kb.dma("sp", wfct[:, :, :], wfc.t.rearrange("(c p) j -> p c j", p=128), reads=[wfc], writes=[wfct])
        for p in range(NPASS):
            t0 = p * T
            with contextlib.ExitStack() as st:
                hT = P.sb(st, "bhT", [128, 16, W], BF16)
                hTb1 = Buf("bhTb")
                with contextlib.ExitStack() as s2:
                    G, SH = make_G_SH(P, s2, modb, gmix, l, 0)
                    if p == 0:
                        blocks = [(xf.rows(half - 2, 2), xf.buf, 2, 0, False)]
                    else:
                        blocks = [(xh.t[t0 - 2:t0, :], xh, 2, 0, False)]
                    for tb in range(NB):
                        blocks.append((xh.t[t0 + tb * 128:t0 + (tb + 1) * 128, :], xh, 128, 2 + tb * 128, False))
                    norm_pass(P, s2, blocks, hT, lambda c: hTb1, G, SH, pb[0:4])
                    kb.barrier()
                with contextlib.ExitStack() as s1:
                    mT = P.sb(s1, "mT", [128, 16, W], BF16)
                    with contextlib.ExitStack() as s2:
                        obS = P.sb(s2, "obS", [128, 24, W], BF16)
                        with contextlib.ExitStack() as s3:
                            ta = [P.sb(s3, "ta%d" % i, [128, 4, W], BF16) for i in range(3)]
                            tb_ = [P.sb(s3, "tb%d" % i, [128, 4, W], BF16) for i in range(3)]
                            tc = [P.sb(s3, "tc%d" % i, [128, 4, W], BF16) for i in range(3)]
                            n = 0
                            for br in range(3):
                                for r in range(2):
                                    i = n % 3; n += 1
                                    SR = ob_chunk(S)
                                    nsub, cps = 512 // SR, SR // 128
                                    for sub in range(nsub):
                                        j = (br * 512) // SR + sub
                                        rows = obG.t[j * 2 * SR + r * SR:j * 2 * SR + (r + 1) * SR, :].rearrange("(c p) s -> p c s", p=128)
                                        cc = slice(sub * cps, (sub + 1) * cps)
                                        pt = sub > 0
                                        kb.dma("sp", ta[i][:, cc, 2:W], rows[:, :, t0:t0 + T], reads=[obG], writes=[ta[i]], part=pt)
                                        kb.dma("sp", tb_[i][:, cc, 2:W], rows[:, :, half + t0:half + t0 + T], reads=[obG], writes=[tb_[i]], part=pt)
                                        if p == 0:
                                            kb.dma("sp", ta[i][:, cc, 0:2], rows[:, :, half - 2:half], reads=[obG], writes=[ta[i]], part=True)
                                            kb.dma("sp", tb_[i][:, cc, 0:2], rows[:, :, half - 2:half], reads=[obG], writes=[tb_[i]], part=True)
                                        else:
                                            kb.dma("sp", ta[i][:, cc, 0:2], rows[:, :, t0 - 2:t0], reads=[obG], writes=[ta[i]], part=True)
                                            kb.dma("sp", tb_[i][:, cc, 0:2], rows[:, :, half + t0 - 2:half + t0], reads=[obG], writes=[tb_[i]], part=True)
                                    P.act(tc[i][:, :, :], ta[i][:, :, :], AF.Copy, [ta[i], selt], [tc[i]], scale=selt[:, 0:1])
                                    P.stt(obS[:, br * 8 + r * 4:br * 8 + r * 4 + 4, :], tb_[i][:, :, :], selt[:, 1:2], tc[i][:, :, :], ALU.mult, ALU.add,
                                          [tb_[i], selt, tc[i]], [obS])
                            kb.barrier()
                        wg = [P.sb(s2, "wg%d" % i, [128, 3, 16, 256], BF16) for i in range(2)]
                        wbr = [P.sb(s2, "wbr%d" % i, [128, 3, 8, 256], BF16) for i in range(2)]
                        Gs = [P.sb(s2, "Gs%d" % i, [128, TT], F32) for i in range(2)]
                        ac = [P.sb(s2, "ac%d" % i, [128, TT], F32) for i in range(2)]
                        tp = [P.sb(s2, "tp%d" % i, [128, TT], F32) for i in range(2)]
                        tiles = col_tiles(W, TT)
                        it = 0
                        for cg in range(8):
                            w_g, w_b = wg[cg % 2], wbr[cg % 2]
                            for br in range(3):
                                src = wG.t.rearrange("(kc p) n -> p kc n", p=128)[:, :, br * 2048 + cg * 256:br * 2048 + (cg + 1) * 256]
                                kb.dma("pool", w_g[:, br, :, :], src, reads=[wG], writes=[w_g], part=br > 0)
                                src = wBr.t.rearrange("(kc p) n -> p kc n", p=128)[:, br * 8:(br + 1) * 8, cg * 256:(cg + 1) * 256]
                                kb.dma("pool", w_b[:, br, :, :], src, reads=[wBr], writes=[w_b], part=br > 0)
                            for cb in range(2):
                                cs = slice(cb * 128, (cb + 1) * 128)
                                mc = cg * 2 + cb
                                for (c0, w) in tiles:
                                    i = it % 2; it += 1
                                    for br in range(3):
                                        pg, py = pb[(2 * br) % 4 + 0 if False else (br % 2) * 2], pb[(br % 2) * 2 + 1]
                                        for kc in range(16):
                                            P.mm(pg[:, 0:w], w_g[:, br, kc, cs], hT[:, kc, c0:c0 + w], kc == 0, kc == 15, [w_g, hTb1], [pg])
                                        for kc in range(8):
                                            P.mm(py[:, 0:w], w_b[:, br, kc, cs], obS[:, br * 8 + kc, c0:c0 + w], kc == 0, kc == 7, [w_b, obS], [py])
                                        P.act(Gs[i][:, 0:w], pg[:, 0:w], AF.Sigmoid, [pg], [Gs[i]])
                                        if br == 0:
                                            P.tt(ac[i][:, 0:w], Gs[i][:, 0:w], py[:, 0:w], ALU.mult, [Gs[i], py], [ac[i]])
                                        else:
                                            P.tt(tp[i][:, 0:w], Gs[i][:, 0:w], py[:, 0:w], ALU.mult, [Gs[i], py], [tp[i]])
                                            if br == 1:
                                                P.tt(ac[i][:, 0:w], ac[i][:, 0:w], tp[i][:, 0:w], ALU.add, [ac[i], tp[i]], [ac[i]])
                                            else:
                                                P.tt(mT[:, mc, c0:c0 + w], ac[i][:, 0:w], tp[i][:, 0:w], ALU.add, [ac[i], tp[i]], [mT])
                        kb.barrier()
                    with contextlib.ExitStack() as s2:
                        wo = P.sb(s2, "wo", [128, 16, D], BF16)
                        GT = P.sb(s2, "GT1", [128, D], F32)
                        xt = [P.sb(s2, "bxt%d" % i, [128, D], F32) for i in range(2)]
                        xm = [P.sb(s2, "bxm%d" % i, [128, D], F32) for i in range(2)]
                        load_bcast(P, GT, modb.t[2, :], [modb])
                        for q in range(4):
                            src = wO.t.rearrange("(kc p) n -> p kc n", p=128)[:, :, q * 512:(q + 1) * 512]
                            kb.dma("pool", wo[:, :, q * 512:(q + 1) * 512], src, reads=[wO], writes=[wo], part=q > 0)
                        blocks = [(0, 2, (xf.buf, xf.rows(half - 2, 2)) if p == 0 else (xh, xh.t[t0 - 2:t0, :]))]
                        for tb in range(NB):
                            blocks.append((2 + tb * 128, 128, (xh, xh.t[t0 + tb * 128:t0 + (tb + 1) * 128, :])))
                        for bi, (c0, n_, (xsrc, xap)) in enumerate(blocks):
                            i = bi % 2
                            kb.dma("sp", xt[i][0:n_, :], xap, reads=[xsrc], writes=[xt[i]])
                            for q in range(4):
                                po = pb[q]
                                for kc in range(16):
                                    P.mm(po[0:n_, :], mT[:, kc, c0:c0 + n_], wo[:, kc, q * 512:(q + 1) * 512], kc == 0, kc == 15, [mT, wo], [po])
                                qs = slice(q * 512, (q + 1) * 512)
                                P.tt(xm[i][0:n_, qs], po[0:n_, :], GT[0:n_, qs], ALU.mult, [po, GT], [xm[i]])
                                P.tt(xm[i][0:n_, qs], xm[i][0:n_, qs], xt[i][0:n_, qs], ALU.add, [xm[i], xt[i]], [xm[i]])
                            kb.dma("sp", xmid.t[c0:c0 + n_, :], xm[i][0:n_, :], reads=[xm[i]], writes=[xmid], key=xm[i])
                        kb.barrier()
                with contextlib.ExitStack() as s2:
                    G, SH = make_G_SH(P, s2, modb, gffn, l, 1)
                    blocks = [(xmid.t[0:2, :], xmid, 2, 0, False)]
                    for tb in range(NB):
                        blocks.append((xmid.t[2 + tb * 128:2 + (tb + 1) * 128, :], xmid, 128, 2 + tb * 128, False))
                    norm_pass(P, s2, blocks, hT, lambda c: hTb1, G, SH, pb[0:4])
                    kb.barrier()
                with contextlib.ExitStack() as s2:
                    acc = P.sb(s2, "facc", [128, NB, D], F32)
                    wu = [P.sb(s2, "wu%d" % i, [128, 2, 16, 256], BF16) for i in range(2)]
                    wd = [P.sb(s2, "wd%d" % i, [128, 2, D], BF16) for i in range(2)]
                    GP = [P.sb(s2, "GP%d" % i, [128, W], F32) for i in range(2)]
                    CV = [P.sb(s2, "CV%d" % i, [128, T], F32) for i in range(2)]
                    aT = [P.sb(s2, "aT%d" % i, [128, 2, T], BF16) for i in range(2)]
                    tiles = col_tiles(W, TT)
                    mtiles = col_tiles(T, TT)
                    NG = DFF // 256
                    for g in range(NG):
                        w_u, w_d, a_T = wu[g % 2], wd[g % 2], aT[g % 2]
                        for hv in range(2):
                            src = wU.t.rearrange("(kc p) n -> p kc n", p=128)[:, :, hv * DFF + g * 256:hv * DFF + (g + 1) * 256]
                            kb.dma("pool", w_u[:, hv, :, :], src, reads=[wU], writes=[w_u], part=hv > 0)
                        src = wD.t[g * 256:(g + 1) * 256, :].rearrange("(kc p) n -> p kc n", p=128)
                        kb.dma("pool", w_d[:, :, :], src, reads=[wD], writes=[w_d])
                        for cb in range(2):
                            cs = slice(cb * 128, (cb + 1) * 128)
                            ch = g * 2 + cb
                            gp, cv = GP[cb], CV[cb]
                            for ti, (c0, w) in enumerate(tiles):
                                pg = pb[ti % 2]
                                for kc in range(16):
                                    P.mm(pg[:, 0:w], w_u[:, 0, kc, cs], hT[:, kc, c0:c0 + w], kc == 0, kc == 15, [w_u, hTb1], [pg])
                                P.copy(gp[:, c0:c0 + w], pg[:, 0:w], [pg], [gp], eng="act")
                            if p == 0:
                                P.ts(gp[:, 0:2], gp[:, 0:2], selt[:, 1:2], None, ALU.mult, None, [gp, selt], [gp])
                            P.ts(cv[:, :], gp[:, 0:T], wfct[:, ch, 0:1], None, ALU.mult, None, [gp, wfct], [cv])
                            for j in range(1, 3):
                                P.stt(cv[:, :], gp[:, j:T + j], wfct[:, ch, j:j + 1], cv[:, :], ALU.mult, ALU.add, [gp, wfct, cv], [cv])
                            P.act(cv[:, :], cv[:, :], AF.Silu, [cv], [cv])
                            for ti, (c0, w) in enumerate(mtiles):
                                pv = pb[2 + ti % 2]
                                for kc in range(16):
                                    P.mm(pv[:, 0:w], w_u[:, 1, kc, cs], hT[:, kc, 2 + c0:2 + c0 + w], kc == 0, kc == 15, [w_u, hTb1], [pv])
                                P.tt(a_T[:, cb, c0:c0 + w], cv[:, c0:c0 + w], pv[:, 0:w], ALU.mult, [cv, pv], [a_T])
                        for tb in range(NB):
                            for hf in range(2):
                                pd0, pd1 = pb[4 + 2 * ((tb * 2 + hf) % 2)], pb[5 + 2 * ((tb * 2 + hf) % 2)]
                                for q, pd in enumerate((pd0, pd1)):
                                    co = hf * 1024 + q * 512
                                    for kc in range(2):
                                        P.mm(pd[:, :], a_T[:, kc, tb * 128:(tb + 1) * 128], w_d[:, kc, co:co + 512], kc == 0, kc == 1, [a_T, w_d], [pd])
                                    if g == 0:
                                        P.copy(acc[:, tb, co:co + 512], pd[:, :], [pd], [acc], eng="dve")
                                    else:
                                        P.tt(acc[:, tb, co:co + 512], acc[:, tb, co:co + 512], pd[:, :], ALU.add, [acc, pd], [acc])
                    GT = P.sb(s2, "GT2", [128, D], F32)
                    xt = [P.sb(s2, "fxt%d" % i, [128, D], F32) for i in range(2)]
                    load_bcast(P, GT, modb.t[5, :], [modb])
                    for tb in range(NB):
                        i = tb % 2
                        kb.dma("sp", xt[i][:, :], xmid.t[2 + tb * 128:2 + (tb + 1) * 128, :], reads=[xmid], writes=[xt[i]])
                        P.tt(acc[:, tb, :], acc[:, tb, :], GT[:, :], ALU.mult, [acc, GT], [acc])
                        P.tt(xt[i][:, :], xt[i][:, :], acc[:, tb, :], ALU.add, [xt[i], acc], [xt[i]])
                        kb.dma("sp", xn.t[t0 + tb * 128:t0 + (tb + 1) * 128, :], xt[i][:, :], reads=[xt[i]], writes=[xn], key=xt[i])
                    kb.barrier()


def prep_B(inp, l, rank):
    w = inp["w_in"][l]
    sel = np.zeros((128, 2), np.float32)
    sel[:, rank] = 1.0
    return {"wG%d" % l: np.ascontiguousarray(w[:, OFF["gm"]:]), "wBr%d" % l: inp["w_branch"][l].reshape(3072, D),
            "wO%d" % l: inp["w_out"][l], "wU%d" % l: inp["w_up"][l], "wD%d" % l: inp["w_down"][l],
            "wfc%d" % l: np.ascontiguousarray(inp["w_ffconv"][l].T), "sel": sel}


def stageP0(P, modbs, groups8, collective=True):
    kb, pb = P.kb, P.pb
    L = DEPTH
    cm = P.dram("cm", [4, D])
    wa = P.dram("wa", [L, D, 1536])
    ba = P.dram("ba", [L, 1536])
    selb = P.dram("selb", [4, 128])
    sel8 = P.dram("sel8", [4, 8])
    arin = P.dram("arin", [4, L * 8 * 1536])
    aro = P.dram("aro", [4, L * 8 * 1536])
    with contextlib.ExitStack() as st:
        c4 = P.sb(st, "c4", [4, D], F32)
        cT = P.sb(st, "cT", [128, 16, 4], F32)
        s8 = P.sb(st, "s8", [4, 8], F32)
        sb_ = P.sb(st, "selbt", [4, 128], F32)
        msl = P.sb(st, "msl", [4, L * 1536], F32)
        bt = P.sb(st, "bt", [4, L * 1536], F32)
        tmp = [P.sb(st, "p0t%d" % i, [4, L * 1536], F32) for i in range(2)]
        kb.dma("sp", c4[:, :], cm.t[:, :], reads=[cm], writes=[c4])
        kb.dma("sp", s8[:, :], sel8.t[:, :], reads=[sel8], writes=[s8])
        kb.dma("sp", sb_[:, :], selb.t[:, :], reads=[selb], writes=[sb_])
        kb.dma("sp", bt[:, :], ba.t.rearrange("l n -> (l n)").partition_broadcast(4), reads=[ba], writes=[bt])
        P.act(c4[:, :], c4[:, :], AF.Silu, [c4], [c4])
        for kc in range(16):
            pk = pb[kc % 2]
            P.mm(pk[:, 0:4], c4[0:4, kc * 128:(kc + 1) * 128], P.identf[0:4, 0:4], True, True, [c4, P.identf], [pk])
            P.copy(cT[:, kc, :], pk[:, 0:4], [pk], [cT])
        with contextlib.ExitStack() as s2:
            wt = P.sb(s2, "wat", [128, 16, 1536], F32)
            for l in range(L):
                kb.dma("sp", wt[:, :, :], wa.t[l].rearrange("(kc p) n -> p kc n", p=128), reads=[wa], writes=[wt])
                for q in range(3):
                    pk = pb[2 + q % 2]
                    for kc in range(16):
                        P.mm(pk[0:4, :], cT[:, kc, :], wt[:, kc, q * 512:(q + 1) * 512], kc == 0, kc == 15, [cT, wt], [pk])
                    o = l * 1536 + q * 512
                    P.tt(msl[:, o:o + 512], pk[0:4, :], bt[:, o:o + 512], ALU.add, [pk, bt], [msl])
            kb.barrier()
        if not collective:
            mo_ = P.dram("mslo", [4, L * 1536], kind="out")
            kb.dma("sp", mo_.t[:, :], msl[:, :], reads=[msl], writes=[mo_], key=msl)
            kb.wait_all("sp", [mo_])
            return
        av = arin.t.rearrange("b (l s n) -> b l s n", l=L, s=8)
        for s in range(8):
            t = tmp[s % 2]
            P.ts(t[:, :], msl[:, :], s8[:, s:s + 1], None, ALU.mult, None, [msl, s8], [t])
            kb.dma("sp", av[:, :, s, :], t[:, :].rearrange("b (l n) -> b l n", l=L), reads=[t], writes=[arin], key=t)
        kb.barrier()
        kb.coll("AllReduce", ALU.add, groups8, arin.t.opt(), aro.t.opt(), [arin], [aro], aro)
        kb.barrier()
        with contextlib.ExitStack() as s2:
            ar = P.sb(s2, "ar", [4, L * 12288], F32)
            row = [P.sb(s2, "row%d" % i, [1, 2048], F32) for i in range(2)]
            kb.dma("sp", ar[:, :], aro.t[:, :], reads=[aro], writes=[ar])
            n = 0
            for l in range(L):
                for m in range(NMOD):
                    r = row[n % 2]; n += 1
                    for q in range(4):
                        pk = pb[q]
                        o = l * 12288 + m * 2048 + q * 512
                        P.mm(pk[:, :], sb_[0:4, :], ar[0:4, o:o + 512], True, True, [sb_, ar], [pk])
                        P.copy(r[0:1, q * 512:(q + 1) * 512], pk[0:1, :], [pk], [r], eng="act" if q % 2 else "dve")
                    kb.dma("sp", modbs[l].t[m:m + 1, :], r[0:1, :], reads=[r], writes=[modbs[l]], key=r)
            kb.barrier()


def prep_P0(inp, core):
    b = core // 2
    selb = np.zeros((4, 128), np.float32); selb[b, :] = 1.0
    sel8 = np.zeros((4, 8), np.float32); sel8[:, core] = 1.0
    return {"cm": inp["c"], "wa": np.ascontiguousarray(inp["w_ada"][:, :, core * 1536:(core + 1) * 1536]),
            "ba": np.ascontiguousarray(inp["b_ada"][:, core * 1536:(core + 1) * 1536]), "selb": selb, "sel8": sel8}


PAIRS = [[0, 1], [2, 3], [4, 5], [6, 7]]


def build_full(S, use_p0=True, pairs=PAIRS, groups8=None, layers=(0, 1), debug=False):
    half = S // 2
    io = {"xf0": "in", "xh0": "in", "y": "out"}
    if not use_p0:
        io.update({"modb0": "in", "modb1": "in"})
    P = Prog(S, io=io)
    kb = P.kb
    with contextlib.ExitStack() as st:
        P.consts(st)
        gmix = P.dram("gmix", [DEPTH, D])
        gffn = P.dram("gffn", [DEPTH, D])
        modbs = [P.dram("modb%d" % l, [NMOD, D]) for l in range(DEPTH)]
        if use_p0:
            stageP0b(P, modbs, pairs)
        xf = XF(P.dram("xf0", [S, D]), S)
        xh = P.dram("xh0", [half, D])
        for li, l in enumerate(layers):
            last = li == len(layers) - 1
            obT = P.dram("obT%d" % l, [1536, S], BF16)
            obG = P.dram("obG%d" % l, [2 * 1536, S], BF16)
            stageA(P, l, xf, modbs[l], gmix, obT)
            kb.barrier()
            gather_rows(P, obT, obG, 1536, ob_chunk(S), pairs)
            kb.barrier()
            if debug and l == 0:
                d1 = P.dram("dbg_obT", [1536, S], BF16, kind="out")
                d2 = P.dram("dbg_obG", [2 * 1536, S], BF16, kind="out")
                kb.dma("sp", d1.t[:, :], obT.t[:, :], reads=[obT], writes=[d1], key=Buf("kd1"))
                kb.dma("sp", d2.t[:, :], obG.t[:, :], reads=[obG], writes=[d2], key=Buf("kd2"))
                kb.barrier()
            xn = P.dram("y" if last else "xn%d" % l, [half, D])
            stageB(P, l, xh, xf, modbs[l], gmix, gffn, obG, xn)
            kb.barrier()
            if not last:
                xf2 = P.dram("xf%d" % (l + 1), [S, D])
                XR = min(256, half)
                gather_rows(P, xn, xf2, half, XR, pairs)
                kb.barrier()
                if debug:
                    d3 = P.dram("dbg_xn", [half, D], kind="out")
                    d4 = P.dram("dbg_xf", [S, D], kind="out")
                    kb.dma("sp", d3.t[:, :], xn.t[:, :], reads=[xn], writes=[d3], key=Buf("kd3"))
                    kb.dma("sp", d4.t[:, :], xf2.t[:, :], reads=[xf2], writes=[d4], key=Buf("kd4"))
                    kb.barrier()
                xf, xh = XF(xf2, S, XR), xn
        kb.wait_all("sp", [P.drams["y"]])
        kb.emit()
    return P


def prep_core(inp, core, S, use_p0=True, layers=(0, 1), x=None):
    b, r = core // 2, core % 2
    half = S // 2
    x = inp["x"] if x is None else x
    m = {"cst": host_consts(), "gmix": inp["g_mix"], "gffn": inp["g_ffn"],
         "xf0": np.ascontiguousarray(x[b, :S]), "xh0": np.ascontiguousarray(x[b, r * half:(r + 1) * half])}
    for l in layers:
        m.update(prep_A(inp, l, r))
        m.update(prep_B(inp, l, r))
    if use_p0:
        m.update(prep_P0b(inp, core))
    return m


_CACHE = {}


def build_stage(S, which, l=0):
    half = S // 2
    if which == "P0":
        P = Prog(S)
        with contextlib.ExitStack() as st:
            P.consts(st)
            stageP0(P, None, None, collective=False)
            P.kb.emit()
        return P
    if which == "A":
        P = Prog(S, io={"xf0": "in", "modb%d" % l: "in", "obT%d" % l: "out"})
        with contextlib.ExitStack() as st:
            P.consts(st)
            xf = XF(P.dram("xf0", [S, D]), S)
            modb = P.dram("modb%d" % l, [NMOD, D])
            gmix = P.dram("gmix", [DEPTH, D])
            obT = P.dram("obT%d" % l, [1536, S], BF16)
            stageA(P, l, xf, modb, gmix, obT)
            P.kb.wait_all("sp", [obT])
            P.kb.emit()
        return P
    P = Prog(S, io={"xf0": "in", "xh0": "in", "modb%d" % l: "in", "obG%d" % l: "in", "y": "out"})
    with contextlib.ExitStack() as st:
        P.consts(st)
        xf = XF(P.dram("xf0", [S, D]), S)
        xh = P.dram("xh0", [half, D])
        modb = P.dram("modb%d" % l, [NMOD, D])
        gmix = P.dram("gmix", [DEPTH, D])
        gffn = P.dram("gffn", [DEPTH, D])
        obG = P.dram("obG%d" % l, [2 * 1536, S], BF16)
        y = P.dram("y", [half, D])
        stageB(P, l, xh, xf, modb, gmix, gffn, obG, y)
        P.kb.wait_all("sp", [y])
        P.kb.emit()
    return P


def _get(key, fn):
    if key not in _CACHE:
        _CACHE[key] = fn()
    return _CACHE[key]


def kernel_unfused(inp):
    S = inp["x"].shape[1]
    half = S // 2
    cores = list(range(8))
    cst = host_consts()
    P = _get((S, "P0"), lambda: build_stage(S, "P0"))
    maps = []
    for c in cores:
        m = {"cst": cst}
        m.update(prep_P0(inp, c))
        maps.append(m)
    res = run_bass_kernel_spmd(P.nc, maps, core_ids=cores)
    msl = [np.asarray(res.results[c]["mslo"]).reshape(4, DEPTH, 1536) for c in cores]
    mod = np.concatenate(msl, axis=2).reshape(4, DEPTH, NMOD, D)
    x = inp["x"]
    SR = ob_chunk(S)
    for l in range(DEPTH):
        P = _get((S, "A", l), lambda: build_stage(S, "A", l))
        maps = []
        for c in cores:
            b, r = c // 2, c % 2
            m = {"cst": cst, "gmix": inp["g_mix"], "xf0": np.ascontiguousarray(x[b]), "modb%d" % l: np.ascontiguousarray(mod[b, l])}
            m.update(prep_A(inp, l, r))
            maps.append(m)
        res = run_bass_kernel_spmd(P.nc, maps, core_ids=cores)
        obT = [np.asarray(res.results[c]["obT%d" % l]) for c in cores]
        P = _get((S, "B", l), lambda: build_stage(S, "B", l))
        maps = []
        for c in cores:
            b, r = c // 2, c % 2
            obG = np.concatenate([obT[2 * b + rr][j * SR:(j + 1) * SR] for j in range(1536 // SR) for rr in range(2)], axis=0)
            m = {"cst": cst, "gmix": inp["g_mix"], "gffn": inp["g_ffn"], "xf0": np.ascontiguousarray(x[b]),
                 "xh0": np.ascontiguousarray(x[b, r * half:(r + 1) * half]), "modb%d" % l: np.ascontiguousarray(mod[b, l]),
                 "obG%d" % l: obG}
            m.update(prep_B(inp, l, r))
            maps.append(m)
        res = run_bass_kernel_spmd(P.nc, maps, core_ids=cores)
        out = np.empty((4, S, D), np.float32)
        for c in cores:
            b, r = c // 2, c % 2
            out[b, r * half:(r + 1) * half] = np.asarray(res.results[c]["y"])
        x = out
    return x


def kernel_fused(inp):
    S = inp["x"].shape[1]
    half = S // 2
    P = _get((S, "full"), lambda: build_full(S))
    in_maps = [prep_core(inp, c, S) for c in range(8)]
    res = run_bass_kernel_spmd(P.nc, in_maps, core_ids=list(range(8)))
    out = np.empty((4, S, D), np.float32)
    for c in range(8):
        b, r = c // 2, c % 2
        out[b, r * half:(r + 1) * half] = np.asarray(res.results[c]["y"])
    return out


FUSED = True


def kernel(**inputs):
    inp = {k: np.asarray(v) for k, v in inputs.items()}
    return kernel_fused(inp) if FUSED else kernel_unfused(inp)


def stageP0b(P, modbs, pairs):
    kb, pb = P.kb, P.pb
    L = DEPTH
    HC = 6144
    cb = P.dram("cmb", [1, D])
    wa = P.dram("wah", [L, D, HC])
    ba = P.dram("bah", [1, L * HC])
    msd = P.dram("msd", [1, L * HC])
    mga = P.dram("mga", [2, L * HC])
    with contextlib.ExitStack() as st:
        c1 = P.sb(st, "c1", [1, D], F32)
        cT = P.sb(st, "cT1", [128, 16], F32)
        wt = [P.sb(st, "wat%d" % i, [128, 16, 512], F32) for i in range(2)]
        bt = [P.sb(st, "bt1%d" % i, [1, 512], F32) for i in range(2)]
        ot = [P.sb(st, "ot1%d" % i, [1, 512], F32) for i in range(2)]
        kb.dma("sp", c1[:, :], cb.t[:, :], reads=[cb], writes=[c1])
        P.act(c1[:, :], c1[:, :], AF.Silu, [c1], [c1])
        for kc in range(16):
            pk = pb[kc % 2]
            P.mm(pk[:, 0:1], c1[0:1, kc * 128:(kc + 1) * 128], P.identf[0:1, 0:1], True, True, [c1, P.identf], [pk])
            P.copy(cT[:, kc:kc + 1], pk[:, 0:1], [pk], [cT])
        n = 0
        for l in range(L):
            for q in range(HC // 512):
                i = n % 2; n += 1
                w = wt[i]
                src = wa.t[l].rearrange("(kc p) n -> p kc n", p=128)[:, :, q * 512:(q + 1) * 512]
                kb.dma("sp" if i else "act", w[:, :, :], src, reads=[wa], writes=[w])
                o = l * HC + q * 512
                kb.dma("sp", bt[i][:, :], ba.t[:, o:o + 512], reads=[ba], writes=[bt[i]])
                pk = pb[2 + n % 4]
                for kc in range(16):
                    P.mm(pk[0:1, :], cT[:, kc:kc + 1], w[:, kc, :], kc == 0, kc == 15, [cT, w], [pk])
                P.tt(ot[i][:, :], pk[0:1, :], bt[i][:, :], ALU.add, [pk, bt[i]], [ot[i]])
                kb.dma("sp", msd.t[:, o:o + 512], ot[i][:, :], reads=[ot[i]], writes=[msd], key=ot[i])
        kb.barrier()
        kb.coll("AllGather", ALU.bypass, pairs, msd.t[:, :], mga.t[:, :], [msd], [mga], mga)
        kb.barrier()
        stg = [P.sb(st, "mstg%d" % i, [1, HC], F32) for i in range(2)]
        n = 0
        for l in range(L):
            for r in range(2):
                s_ = stg[n % 2]; n += 1
                kb.dma("sp", s_[:, :], mga.t[r:r + 1, l * HC:(l + 1) * HC], reads=[mga], writes=[s_])
                dst = modbs[l].t.rearrange("m d -> (m d)")[r * HC:(r + 1) * HC].rearrange("(o n) -> o n", o=1)
                kb.dma("sp", dst, s_[:, :], reads=[s_], writes=[modbs[l]], key=s_)
        kb.barrier()


def prep_P0b(inp, core):
    b, r = core // 2, core % 2
    return {"cmb": np.ascontiguousarray(inp["c"][b:b + 1]),
            "wah": np.ascontiguousarray(inp["w_ada"][:, :, r * 6144:(r + 1) * 6144]),
            "bah": np.ascontiguousarray(inp["b_ada"][:, r * 6144:(r + 1) * 6144]).reshape(1, -1)}
```
